# Optimizing a Trainium2 kernel written in Bass

```python
import jax, jax.numpy as jnp
from jax import lax
import numpy as np

D_MODEL = 2048
BATCH = 2
SEQ = 16384
DEPTH = 2
DEC_BATCH = 4
DEC_SEQ = 4096
PAST_LEN = 128

GRID_W = 64
POOL_WINDOWS = (2, 4, 8, 16)
N_POOL_GROUPS = len(POOL_WINDOWS)
POOL_GROUP_DIM = D_MODEL // 8
POOL_DIM = N_POOL_GROUPS * POOL_GROUP_DIM
N_HEADS = 16
HEAD_DIM = D_MODEL // 32
ATTN_DIM = N_HEADS * HEAD_DIM
WIN_ROWS = 8
WIN_COLS = 16
N_BRANCH = 2
IN_DIM = POOL_DIM + 3 * ATTN_DIM + N_BRANCH * D_MODEL
D_FF = 256 * ((8 * D_MODEL // 3 + 255) // 256)
CONV_WIDTH = 3
EPS = 1e-6

kernel_name = "hybrid_pool_natten_convglu_encoder"


def rmsnorm(x, g):
    xf = x.astype(jnp.float32)
    y = xf * lax.rsqrt(jnp.mean(xf * xf, axis=-1, keepdims=True) + EPS)
    return (y * g.astype(jnp.float32)).astype(x.dtype)


def pool_mixer(z, pool_w, pool_scale):
    n = z.shape[1]
    zf = z.astype(jnp.float32)
    cs = jnp.concatenate([jnp.zeros_like(zf[:, :1]), jnp.cumsum(zf, axis=1)], axis=1)
    t = jnp.arange(n)
    outs = []
    for g, w in enumerate(POOL_WINDOWS):
        sl = slice(g * POOL_GROUP_DIM, (g + 1) * POOL_GROUP_DIM)
        lo = jnp.clip(t - w // 2, 0, n - 1)
        hi = jnp.clip(t + (w - w // 2) - 1, 0, n - 1)
        csg = cs[:, :, sl]
        win_sum = jnp.take(csg, hi + 1, axis=1) - jnp.take(csg, lo, axis=1)
        cnt = (hi - lo + 1).astype(jnp.float32)[None, :, None]
        pooled = (win_sum / cnt - zf[:, :, sl]).astype(z.dtype)
        outs.append(jnp.einsum('bnc,cd->bnd', pooled, pool_w[g]))
    return jnp.concatenate(outs, axis=-1) * pool_scale


def neighbourhood_attention(q, k, v, rpb):
    B, n, H, Dh = q.shape
    rows = n // GRID_W
    kr = min(WIN_ROWS, rows)
    scale = HEAD_DIM ** -0.5
    qg = q.reshape(B, rows, GRID_W, H, Dh)
    kg = k.reshape(B, rows, GRID_W, H, Dh)
    vg = v.reshape(B, rows, GRID_W, H, Dh)
    row_ids = jnp.arange(rows)
    row_start = jnp.clip(row_ids - kr // 2, 0, rows - kr)
    cols = jnp.arange(GRID_W)
    col_start = jnp.clip(cols - WIN_COLS // 2, 0, GRID_W - WIN_COLS)
    col_idx = col_start[:, None] + jnp.arange(WIN_COLS)[None, :]
    dc = col_idx - cols[:, None]

    def row_step(args):
        q_row, s, r = args
        k_blk = lax.dynamic_slice_in_dim(kg, s, kr, axis=1)
        v_blk = lax.dynamic_slice_in_dim(vg, s, kr, axis=1)
        k_sel = k_blk[:, :, col_idx]
        v_sel = v_blk[:, :, col_idx]
        dr = s + jnp.arange(kr) - r
        bias = rpb[:, dr + WIN_ROWS - 1][:, :, dc + WIN_COLS - 1]
        bias = jnp.transpose(bias, (0, 2, 1, 3)).astype(jnp.float32)
        scores = jnp.einsum('bchd,bicjhd->bhcij', q_row, k_sel,
                            preferred_element_type=jnp.float32) * scale + bias[None]
        p = jax.nn.softmax(scores.reshape(B, H, GRID_W, kr * WIN_COLS), axis=-1)
        p = p.reshape(B, H, GRID_W, kr, WIN_COLS).astype(v.dtype)
        return jnp.einsum('bhcij,bicjhd->bchd', p, v_sel)

    q_rows = jnp.transpose(qg, (1, 0, 2, 3, 4))
    out = lax.map(row_step, (q_rows, row_start, row_ids))
    return jnp.transpose(out, (1, 0, 2, 3, 4)).reshape(B, n, H * Dh)


def mixer_block(u, w_in, pool_w, pool_scale, rpb, w_pool_br, w_attn_br, w_out):
    B, n, _ = u.shape
    proj = u @ w_in
    splits = [POOL_DIM, POOL_DIM + ATTN_DIM, POOL_DIM + 2 * ATTN_DIM,
              POOL_DIM + 3 * ATTN_DIM, POOL_DIM + 3 * ATTN_DIM + D_MODEL]
    z_pool, q, k, v, g_pool, g_attn = jnp.split(proj, splits, axis=-1)
    pool_out = pool_mixer(z_pool, pool_w, pool_scale) @ w_pool_br
    attn = neighbourhood_attention(q.reshape(B, n, N_HEADS, HEAD_DIM),
                                   k.reshape(B, n, N_HEADS, HEAD_DIM),
                                   v.reshape(B, n, N_HEADS, HEAD_DIM), rpb)
    attn_out = attn @ w_attn_br
    merged = jax.nn.sigmoid(g_pool) * pool_out + jax.nn.sigmoid(g_attn) * attn_out
    return merged @ w_out


def conv_glu_ffn(h, w_up, conv_w, conv_b, w_down):
    a = h @ w_up
    ap = jnp.pad(a, ((0, 0), (1, 1), (0, 0)))
    a = ap[:, :-2] * conv_w[0] + ap[:, 1:-1] * conv_w[1] + ap[:, 2:] * conv_w[2] + conv_b
    gate, val = jnp.split(a, 2, axis=-1)
    return (jax.nn.gelu(gate, approximate=False) * val) @ w_down


def trunk(x, norm1_g, w_in, pool_w, pool_scale, rpb, w_pool_br, w_attn_br, w_out,
          norm2_g, w_up, conv_w, conv_b, w_down, norm_f):
    for l in range(DEPTH):
        x = x + mixer_block(rmsnorm(x, norm1_g[l]), w_in[l], pool_w[l], pool_scale[l],
                            rpb[l], w_pool_br[l], w_attn_br[l], w_out[l])
        x = x + conv_glu_ffn(rmsnorm(x, norm2_g[l]), w_up[l], conv_w[l], conv_b[l], w_down[l])
    return rmsnorm(x, norm_f)


def setup_inputs(seed: int = 0) -> dict:
    key = jax.random.key(seed)
    ks = jax.random.split(key, 16)
    f32 = jnp.float32
    L = DEPTH
    nrm = lambda k, shape, s: jax.random.normal(k, shape, f32) * s
    return {
        "x_prompt": nrm(ks[0], (BATCH, SEQ, D_MODEL), 1.0),
        "x_sample": nrm(ks[1], (DEC_BATCH, DEC_SEQ, D_MODEL), 1.0),
        "norm1_g": 1.0 + nrm(ks[2], (L, D_MODEL), 0.05),
        "w_in": nrm(ks[3], (L, D_MODEL, IN_DIM), D_MODEL ** -0.5),
        "pool_w": nrm(ks[4], (L, N_POOL_GROUPS, POOL_GROUP_DIM, POOL_GROUP_DIM), POOL_GROUP_DIM ** -0.5),
        "pool_scale": 1.0 + nrm(ks[5], (L, POOL_DIM), 0.1),
        "rpb": nrm(ks[6], (L, N_HEADS, 2 * WIN_ROWS - 1, 2 * WIN_COLS - 1), 0.1),
        "w_pool_br": nrm(ks[7], (L, POOL_DIM, D_MODEL), POOL_DIM ** -0.5),
        "w_attn_br": nrm(ks[8], (L, ATTN_DIM, D_MODEL), ATTN_DIM ** -0.5),
        "w_out": nrm(ks[9], (L, D_MODEL, D_MODEL), D_MODEL ** -0.5),
        "norm2_g": 1.0 + nrm(ks[10], (L, D_MODEL), 0.05),
        "w_up": nrm(ks[11], (L, D_MODEL, 2 * D_FF), D_MODEL ** -0.5),
        "conv_w": nrm(ks[12], (L, CONV_WIDTH, 2 * D_FF), CONV_WIDTH ** -0.5),
        "conv_b": nrm(ks[13], (L, 2 * D_FF), 0.02),
        "w_down": nrm(ks[14], (L, D_FF, D_MODEL), D_FF ** -0.5),
        "norm_f": 1.0 + nrm(ks[15], (D_MODEL,), 0.05),
    }


def reference(x_prompt, x_sample, norm1_g, w_in, pool_w, pool_scale, rpb, w_pool_br, w_attn_br,
              w_out, norm2_g, w_up, conv_w, conv_b, w_down, norm_f):
    y_prompt = trunk(x_prompt, norm1_g, w_in, pool_w, pool_scale, rpb, w_pool_br, w_attn_br,
                     w_out, norm2_g, w_up, conv_w, conv_b, w_down, norm_f)
    y_sample = trunk(x_sample, norm1_g, w_in, pool_w, pool_scale, rpb, w_pool_br, w_attn_br,
                     w_out, norm2_g, w_up, conv_w, conv_b, w_down, norm_f)
    return (y_prompt, y_sample)
```

```python
import numpy as np
from contextlib import ExitStack
import concourse.bass as bass
import concourse.mybir as mybir
from concourse.bass_utils import run_bass_kernel_spmd

F32 = mybir.dt.float32
BF16 = mybir.dt.bfloat16
AF = mybir.ActivationFunctionType
ALU = mybir.AluOpType
AX = mybir.AxisListType

D = 2048
KC = 16
NL = 2
GW = 64
NH = 16
FF = 5632
FC = 44
NG = 4
GC = 11
HALO = 12
EPS = 1e-6
NEG = -30000.0
POOL_W = (2, 4, 8, 16)


def I(method, *args, **kwargs):
    return lambda e: getattr(e, method)(*args, **kwargs)


class Op:
    __slots__ = ("eng", "fn", "deps", "chan", "token", "has_dep", "inc")

    def __init__(self, eng, fn, chan):
        self.eng = eng
        self.fn = fn
        self.chan = chan
        self.deps = {}
        self.token = None
        self.has_dep = False
        self.inc = False


class Sched:
    ENGS = ("pe", "act", "dve", "pool", "sp")

    def __init__(self, nc, es):
        self.nc = nc
        self.es = es
        self.ops = {e: [] for e in self.ENGS}
        self.last_write = {}
        self.readers = {}
        self.chan_cnt = {}
        self.chan_last = {}
        self.sems = {}
        self.bg_chans = set()

    def sem(self, name):
        if name not in self.sems:
            self.sems[name] = self.es.enter_context(self.nc.semaphore("s_" + name))
        return self.sems[name]

    @staticmethod
    def _key(op):
        return op.chan if op.chan is not None else op.eng

    def _add_dep(self, o, d):
        if d is None or d is o:
            return
        k = self._key(d)
        o.deps[(k, id(d))] = d

    def op(self, eng, fn, reads=(), writes=(), chan=None, extra=()):
        o = Op(eng, fn, chan)
        for r in reads:
            self._add_dep(o, self.last_write.get(r))
        for w in writes:
            self._add_dep(o, self.last_write.get(w))
            for d in self.readers.get(w, {}).values():
                self._add_dep(o, d)
        for d in extra:
            self._add_dep(o, d)
        for d in o.deps.values():
            d.has_dep = True
        for r in reads:
            self.readers.setdefault(r, {})[self._key(o)] = o
        for w in writes:
            self.last_write[w] = o
            self.readers[w] = {}
        if chan is not None:
            n = self.chan_cnt.get(chan, 0) + 1
            self.chan_cnt[chan] = n
            o.token = ("c_" + chan, 16 * n)
            self.chan_last[chan] = o
        self.ops[eng].append(o)
        return o

    def barrier(self):
        lasts = []
        for e in self.ENGS:
            for o in reversed(self.ops[e]):
                if o.fn is not None and o.chan is None:
                    lasts.append(o)
                    break
        for c, o in self.chan_last.items():
            if c not in self.bg_chans:
                lasts.append(o)
        for e in self.ENGS:
            b = Op(e, None, None)
            for d in lasts:
                if d.eng == e and d.chan is None:
                    continue
                self._add_dep(b, d)
                d.has_dep = True
            self.ops[e].append(b)
        self.last_write = {}
        self.readers = {}

    def check(self):
        sem = {}
        pos = {e: 0 for e in self.ENGS}
        progress = True
        while progress:
            progress = False
            for e in self.ENGS:
                lst = self.ops[e]
                while pos[e] < len(lst):
                    o = lst[pos[e]]
                    ok = True
                    for d in o.deps.values():
                        if d.chan is None and d.eng == "pe" and e == "pe":
                            continue
                        if d.token is None:
                            continue
                        s, v = d.token
                        if sem.get(s, 0) < v:
                            ok = False
                            break
                    if not ok:
                        break
                    if o.fn is not None:
                        if o.chan is not None:
                            sem["c_" + o.chan] = sem.get("c_" + o.chan, 0) + 16
                        elif o.inc:
                            sem["e_" + e] = sem.get("e_" + e, 0) + 1
                    pos[e] += 1
                    progress = True
        stuck = {e: (pos[e], len(self.ops[e])) for e in self.ENGS if pos[e] < len(self.ops[e])}
        if stuck:
            raise RuntimeError(f"semaphore protocol deadlock: {stuck}")
        self.n_ops = {e: len(self.ops[e]) for e in self.ENGS}

    def emit(self):
        nc = self.nc
        for e in self.ENGS:
            cnt = 0
            for o in self.ops[e]:
                if o.chan is None and o.fn is not None and o.has_dep:
                    cnt += 1
                    o.token = ("e_" + e, cnt)
                    o.inc = True
        self.check()
        for e in self.ENGS:
            self.sem("e_" + e)
        for c in self.chan_cnt:
            self.sem("c_" + c)
        sched = self

        def body_for(ename):
            def body(e):
                waited = {}
                for o in sched.ops[ename]:
                    need = {}
                    for d in o.deps.values():
                        if d.chan is None and d.eng == "pe" and ename == "pe":
                            continue
                        if d.token is None:
                            continue
                        s, v = d.token
                        if need.get(s, 0) < v:
                            need[s] = v
                    for s, v in need.items():
                        if waited.get(s, 0) < v:
                            e.wait_ge(sched.sems[s], v)
                            waited[s] = v
                    if o.fn is not None:
                        ins = o.fn(e)
                        if o.chan is not None:
                            ins.then_inc(sched.sems["c_" + o.chan], 16)
                        elif o.inc:
                            ins.then_inc(sched.sems["e_" + ename], 1)
            return body

        with nc.Block() as block:
            block.tensor(body_for("pe"))
            block.scalar(body_for("act"))
            block.vector(body_for("dve"))
            block.gpsimd(body_for("pool"))
            block.sync(body_for("sp"))


class Arena:
    def __init__(self, ap, words):
        self.ap = ap
        self.words = words
        self.base = 0
        self.pos = 0
        self.uid = 0

    def set_base(self):
        self.base = self.pos

    def reset(self):
        self.pos = self.base

    def f32(self, n):
        a = self.pos
        self.pos += n
        assert self.pos <= self.words, f"SBUF arena overflow {self.pos} > {self.words}"
        return self.ap[:, a:a + n]

    def bf16(self, n):
        w = (n + 1) // 2
        a = self.pos
        self.pos += w
        assert self.pos <= self.words, f"SBUF arena overflow {self.pos} > {self.words}"
        return self.ap[:, a:a + w].bitcast(BF16)[:, 0:n]

    def name(self, s):
        self.uid += 1
        return f"{s}#{self.uid}"


class Stream:
    def __init__(self, sched, name, slots, srcs, depth, eng="sp"):
        self.s = sched
        self.name = name
        self.slots = slots
        self.srcs = srcs
        self.depth = min(depth, len(slots) - 1)
        self.nxt = 0
        self.eng = eng

    def _issue(self, i):
        k = i % len(self.slots)
        slot = self.slots[k]
        spec = self.srcs[i]
        parts = spec[0]
        extra = spec[1] if len(spec) > 1 else ()
        for (dst_fn, src) in parts:
            dst = dst_fn(slot)
            self.s.op(self.eng, I("dma_start", out=dst, in_=src),
                      writes=[(self.name, k)], chan=f"{self.name}{k}", extra=extra)

    def get(self, i, oldest=None):
        oldest = i if oldest is None else oldest
        while self.nxt < len(self.srcs) and self.nxt <= i + self.depth:
            assert self.nxt - len(self.slots) < oldest, "stream slot still live"
            self._issue(self.nxt)
            self.nxt += 1
        k = i % len(self.slots)
        return self.slots[k], (self.name, k)


def subchunks(n, mx=512):
    out = []
    a = 0
    while a < n:
        b = min(n, a + mx)
        out.append((a, b - a))
        a = b
    return out


def even_chunks(n, mx):
    k = -(-n // mx)
    base = n // k
    rem = n - base * k
    out = []
    a = 0
    for i in range(k):
        sz = base + (1 if i < rem else 0)
        out.append((a, sz))
        a += sz
    return out


class Geo:
    def __init__(self, RP, RS):
        self.segs = []
        off = 0
        for nm, R in (("P", RP), ("S", RS)):
            E = R + 2 * HALO
            self.segs.append(dict(name=nm, R=R, E=E, off=off))
            off += E * GW
        self.NT = off


def build_program(RP, RS, debug=False, stop_after=None):
    geo = Geo(RP, RS)
    NT = geo.NT
    nc = bass.Bass("TRN2", target_bir_lowering=False)
    es = ExitStack()

    def din(name, shape, dt=F32):
        return nc.dram_tensor(name, list(shape), dt, kind="ExternalInput").ap()

    def dscr(name, shape, dt):
        if debug:
            return nc.dram_tensor(name, list(shape), dt, kind="ExternalOutput").ap()
        return nc.dram_tensor(name, list(shape), dt).ap()

    x_ext = din("x_ext", [NT, D])
    valid = din("valid", [1, NT])
    icorr = din("icorr", [128, 2 * 2 * 4 * 2 * 8])
    amask = din("amask", [2, 4, 128, 7 * 128])
    rpb_exp = din("rpb_exp", [NL, NH, 128, 7 * 128])
    rpb_int = din("rpb_int", [NL, NH, 128, 5 * 128])
    ident_d = din("ident", [128, 128])
    w_in = din("w_in", [NL, D, 8192])
    w_pb = din("w_pool_br", [NL, 1024, D])
    w_ab = din("w_attn_br", [NL, 1024, D])
    w_out = din("w_out", [NL, D, D])
    w_up = din("w_up", [NL, D, 2 * FF])
    w_dn = din("w_down", [NL, FF, D])
    poolw_d = din("poolw", [128, NL * 4 * 2 * 256])
    g1_d = din("g1", [128, NL * KC])
    g2_d = din("g2", [128, NL * KC])
    gf_d = din("gf", [128, KC])
    pscale_d = din("pscale", [128, NL * 8])
    convw_d = din("convw", [128, NL * 3 * 88])
    convb_d = din("convb", [128, NL * 88])

    y_p = nc.dram_tensor("y_p", [RP * GW, D], F32, kind="ExternalOutput").ap()
    y_s = nc.dram_tensor("y_s", [RS * GW, D], F32, kind="ExternalOutput").ap()
    youts = [y_p, y_s]

    XT0 = dscr("XT0", [KC, 128, NT], F32)
    XM = dscr("XM", [KC, 128, NT], F32)
    UT = dscr("UT", [KC, 128, NT], BF16)
    ZT = dscr("ZT", [8, 128, NT], F32)
    QT = dscr("QT", [8, 128, NT], BF16)
    KT = dscr("KT", [8, 128, NT], BF16)
    VV = dscr("VV", [NT, 1024], BF16)
    PM = dscr("PM", [8, 128, NT], BF16)
    AT = dscr("AT", [8, 128, NT], BF16)
    WA = nc.dram_tensor("WA", [NL, 24, 128, KC, 128], BF16).ap()
    WV = nc.dram_tensor("WV", [NL, 128, KC, 1024], BF16).ap()
    WG = nc.dram_tensor("WG", [NL, 32, 128, KC, 128], BF16).ap()
    WPB = nc.dram_tensor("WPB", [NL, 16, 128, 8, 128], BF16).ap()
    WAB = nc.dram_tensor("WAB", [NL, 16, 128, 8, 128], BF16).ap()
    WO = nc.dram_tensor("WO", [NL, 16, 128, KC, 128], BF16).ap()
    WUP = nc.dram_tensor("WUP", [NL, 88, 128, KC, 128], BF16).ap()
    WDN = nc.dram_tensor("WDN", [NL, NG, 16, 128, GC, 128], BF16).ap()

    with es:
        S = Sched(nc, es)
        AW = 51200
        arena_t = es.enter_context(nc.sbuf_tensor("arena", [128, AW], F32))
        ar = Arena(arena_t[:], AW)
        ps_t = es.enter_context(nc.psum_tensor("ps", [128, 8 * 512], F32))
        ps = ps_t[:]

        def bank(b, a=0, n=512):
            return ps[:, b * 512 + a: b * 512 + a + n]

        ident = ar.f32(128)
        ones_bf = ar.bf16(128)
        ident_bf = ar.bf16(128)
        g1 = ar.f32(NL * KC)
        g2 = ar.f32(NL * KC)
        gf = ar.f32(KC)
        pscale = ar.f32(NL * 8)
        convw = ar.f32(NL * 3 * 88)
        convb = ar.f32(NL * 88)
        icorr_t = ar.f32(2 * 2 * 4 * 2 * 8)
        ar.set_base()

        def ld_const(dst, src, nm):
            S.op("sp", I("dma_start", out=dst, in_=src), writes=[nm], chan="const")

        ld_const(ident, ident_d[:, :], "ident")
        ld_const(g1, g1_d[:, :], "g1")
        ld_const(g2, g2_d[:, :], "g2")
        ld_const(gf, gf_d[:, :], "gf")
        ld_const(pscale, pscale_d[:, :], "pscale")
        ld_const(convw, convw_d[:, :], "convw")
        ld_const(convb, convb_d[:, :], "convb")
        ld_const(icorr_t, icorr[:, :], "icorr")
        S.op("dve", I("memset", ones_bf, 1.0), writes=["ones"])

        conv_ops = {}

        def conv(key, dst, src):
            S.bg_chans.add("cv_" + key)
            conv_ops[key] = S.op("pool", I("dma_start", out=dst, in_=src), chan="cv_" + key)

        def cols(w, l, c0, n):
            return w[l, :, c0:c0 + n].rearrange("(kc p) c -> p kc c", p=128)

        for l in range(NL):
            for oc in range(24):
                conv(f"WA{l}", WA[l, oc], cols(w_in, l, oc * 128, 128))
            conv(f"WV{l}", WV[l], cols(w_in, l, 3072, 1024))
            for oc in range(32):
                conv(f"WG{l}", WG[l, oc], cols(w_in, l, 4096 + oc * 128, 128))
            for oc in range(16):
                conv(f"WPB{l}", WPB[l, oc], cols(w_pb, l, oc * 128, 128))
                conv(f"WAB{l}", WAB[l, oc], cols(w_ab, l, oc * 128, 128))
            for oc in range(16):
                conv(f"WO{l}", WO[l, oc], cols(w_out, l, oc * 128, 128))
            for i in range(88):
                conv(f"WUP{l}", WUP[l, i], cols(w_up, l, i * 128, 128))
            for gi in range(NG):
                for oc in range(16):
                    conv(f"WDN{l}", WDN[l, gi, oc],
                         w_dn[l, gi * GC * 128:(gi + 1) * GC * 128, oc * 128:(oc + 1) * 128]
                         .rearrange("(cp p) c -> p cp c", p=128))

        def cdep(key):
            return (conv_ops[key],)

        S.barrier()
        S.op("dve", I("tensor_copy", ident_bf, ident), writes=["identb"])

        def rmsnorm(xT, xres, n, tok0, gain, sqslots, rstd, vmask, psb, out_fn, out_res, tag, masked=True):
            chunks = subchunks(n)
            if masked:
                S.op("sp", I("dma_start", out=vmask[:, 0:n],
                                                  in_=valid[0:1, tok0:tok0 + n].partition_broadcast(128)),
                     writes=[tag + "vm"], chan=tag + "vm")
            for c in range(KC):
                k = c % len(sqslots)
                sq = sqslots[k]
                S.op("act", I("activation", out=sq[:, 0:n], in_=xT[:, c, 0:n], func=AF.Square),
                     reads=[xres], writes=[(tag + "sq", k)])
                for si, (a, m) in enumerate(chunks):
                    S.op("pe", I("matmul",
                        bank(psb + si, 0, m), ones_bf, sq[:, a:a + m], start=(c == 0), stop=(c == KC - 1)),
                        reads=[(tag + "sq", k), "ones"], writes=[("ps", psb + si)])
            for si, (a, m) in enumerate(chunks):
                S.op("act", I("activation",
                    out=rstd[:, a:a + m], in_=bank(psb + si, 0, m), func=AF.Sqrt, scale=1.0 / D, bias=EPS),
                    reads=[("ps", psb + si)], writes=[tag + "rstd"])
            S.op("dve", I("reciprocal", rstd[:, 0:n], rstd[:, 0:n]), reads=[tag + "rstd"], writes=[tag + "rstd"])
            if masked:
                S.op("dve", I("tensor_tensor", rstd[:, 0:n], rstd[:, 0:n], vmask[:, 0:n], ALU.mult),
                     reads=[tag + "rstd", tag + "vm"], writes=[tag + "rstd"])
            for c in range(KC):
                S.op("dve", I("scalar_tensor_tensor",
                    out_fn(c), xT[:, c, 0:n], gain[:, c:c + 1], rstd[:, 0:n], ALU.mult, ALU.mult),
                    reads=[xres, tag + "rstd"], writes=[out_res])

        def phase_A(l, row_ranges):
            ar.reset()
            wv = ar.bf16(KC * 1024).rearrange("p (k f) -> p k f", k=KC)
            xT = ar.f32(KC * 1024).rearrange("p (c n) -> p c n", c=KC)
            uT = ar.bf16(KC * 1024).rearrange("p (c n) -> p c n", c=KC)
            sqs = [ar.bf16(1024) for _ in range(3)]
            rstd = ar.f32(1024)
            vmask = ar.f32(1024)
            wslots = [ar.bf16(KC * 128).rearrange("p (k c) -> p k c", k=KC) for _ in range(4)]
            stz = [ar.f32(1024) for _ in range(2)]
            stb = [ar.bf16(1024) for _ in range(3)]
            stv = [ar.bf16(1024) for _ in range(2)]
            xin = [ar.f32(D) for _ in range(2)] if l == 0 else None
            gain = g1[:, l * KC:(l + 1) * KC]

            S.op("sp", I("dma_start", out=wv, in_=WV[l]), writes=["wv"], chan="wv", extra=cdep(f"WV{l}"))

            tiles = []
            for seg, (ra, rb) in zip(geo.segs, row_ranges):
                r = ra
                while r < rb:
                    nr = min(16, rb - r)
                    tiles.append((seg["off"] + r * GW, nr * GW))
                    r += nr
            srcs = []
            for (t0, n) in tiles:
                for oc in range(24):
                    srcs.append(([((lambda s: s), WA[l, oc])], cdep(f"WA{l}")))
            wst = Stream(S, "aw", wslots, srcs, depth=3)
            widx = 0
            ev = 0
            nst = {"z": 0, "b": 0, "v": 0}
            for ti, (t0, n) in enumerate(tiles):
                chunks = subchunks(n)
                if l == 0:
                    for tb in range(n // 128):
                        k = tb % 2
                        S.op("sp", I("dma_start",
                            out=xin[k], in_=x_ext[t0 + tb * 128:t0 + (tb + 1) * 128, :]),
                            writes=[("xin", k)], chan=f"xin{k}")
                        for c4 in range(4):
                            b = 2 + (c4 % 2)
                            for j in range(4):
                                c = c4 * 4 + j
                                S.op("pe", I("transpose",
                                    bank(b, j * 128, 128), xin[k][:, c * 128:(c + 1) * 128], ident),
                                    reads=[("xin", k), "ident"], writes=[("ps", b)])
                            eng = "act" if (c4 % 2 == 0) else "dve"
                            dst = xT[:, c4 * 4:c4 * 4 + 4, tb * 128:(tb + 1) * 128]
                            src = bank(b, 0, 512).rearrange("p (j t) -> p j t", j=4)
                            if eng == "act":
                                S.op("act", I("activation", out=dst, in_=src, func=AF.Copy),
                                     reads=[("ps", b)], writes=["xT"])
                            else:
                                S.op("dve", I("tensor_copy", dst, src),
                                     reads=[("ps", b)], writes=["xT"])
                    S.op("sp", I("dma_start",
                        out=XT0[:, :, t0:t0 + n].rearrange("c p n -> p c n"), in_=xT[:, :, 0:n]),
                        reads=["xT"], chan="xTst")
                else:
                    S.op("sp", I("dma_start",
                        out=xT[:, :, 0:n], in_=XT0[:, :, t0:t0 + n].rearrange("c p n -> p c n")),
                        writes=["xT"], chan="xTld")
                rmsnorm(xT, "xT", n, t0, gain, sqs, rstd, vmask, 0,
                        lambda c: uT[:, c, 0:n], "uT", "A")
                S.op("sp", I("dma_start",
                    out=UT[:, :, t0:t0 + n].rearrange("c p n -> p c n"), in_=uT[:, :, 0:n]),
                    reads=["uT"], chan="uTst")
                for oc in range(24):
                    w, wres = wst.get(widx)
                    widx += 1
                    if oc < 8:
                        k = nst["z"] % 2
                        nst["z"] += 1
                        stage, sres, chn = stz[k], ("stz", k), f"stz{k}"
                        dst = ZT[oc, :, t0:t0 + n]
                    else:
                        k = nst["b"] % 3
                        nst["b"] += 1
                        stage, sres, chn = stb[k], ("stb", k), f"stb{k}"
                        dst = (QT if oc < 16 else KT)[oc % 8, :, t0:t0 + n]
                    for (a, m) in chunks:
                        b = 4 + (ev % 4)
                        ev += 1
                        for kc in range(KC):
                            S.op("pe", I("matmul",
                                bank(b, 0, m), w[:, kc, :], uT[:, kc, a:a + m], start=(kc == 0), stop=(kc == KC - 1)),
                                reads=[wres, "uT"], writes=[("ps", b)])
                        sc = 0.125 if 8 <= oc < 16 else 1.0
                        if ev % 2 == 0:
                            S.op("act", I("activation",
                                out=stage[:, a:a + m], in_=bank(b, 0, m), func=AF.Copy, scale=sc),
                                reads=[("ps", b)], writes=[sres])
                        else:
                            S.op("dve", I("tensor_scalar",
                                stage[:, a:a + m], bank(b, 0, m), sc, 0.0, ALU.mult, ALU.add),
                                reads=[("ps", b)], writes=[sres])
                    S.op("sp", I("dma_start", out=dst, in_=stage[:, 0:n]),
                         reads=[sres], chan=chn)
                for tb in range(n // 128):
                    k = nst["v"] % 2
                    nst["v"] += 1
                    stage = stv[k]
                    for vc in range(2):
                        b = 4 + (ev % 4)
                        ev += 1
                        for kc in range(KC):
                            S.op("pe", I("matmul",
                                bank(b, 0, 512), uT[:, kc, tb * 128:(tb + 1) * 128], wv[:, kc, vc * 512:(vc + 1) * 512],
                                start=(kc == 0), stop=(kc == KC - 1)),
                                reads=["wv", "uT"], writes=[("ps", b)])
                        if ev % 2 == 0:
                            S.op("act", I("activation",
                                out=stage[:, vc * 512:(vc + 1) * 512], in_=bank(b, 0, 512), func=AF.Copy),
                                reads=[("ps", b)], writes=[("stv", k)])
                        else:
                            S.op("dve", I("tensor_copy",
                                stage[:, vc * 512:(vc + 1) * 512], bank(b, 0, 512)),
                                reads=[("ps", b)], writes=[("stv", k)])
                    S.op("sp", I("dma_start",
                        out=VV[t0 + tb * 128:t0 + (tb + 1) * 128, :], in_=stage[:, 0:1024]),
                        reads=[("stv", k)], chan=f"stv{k}")
            S.barrier()

        def load_table(l, src_d, nj, dst, tag):
            stg = [ar.f32(nj * 128) for _ in range(2)]
            for h in range(NH):
                k = h % 2
                S.op("sp", I("dma_start", out=stg[k], in_=src_d[l, h]),
                     writes=[(tag + "stg", k)], chan=f"{tag}stg{k}")
                S.op("pool", I("tensor_copy", dst[:, h, :], stg[k]),
                     reads=[(tag + "stg", k)], writes=[tag])

        def attn_item(l, qT, qres, qcol, kT, kres, kcol0, vt, vres, vblk0, nj, h, tab, tabres,
                      Ps, rdens, ast, astres, acol, it, pre=None):
            c = h // 2
            hp = (h % 2) * 64
            sset = it % 3
            sb = 2 * sset
            ob = 6 + (it % 2)
            P = Ps[it % 3]
            rden = rdens[it % 2]
            W = nj * 128

            def qk():
                if pre is not None:
                    pre()
                for idx in range(nj):
                    S.op("pe", I("matmul",
                        ps[:, sb * 512 + idx * 128: sb * 512 + (idx + 1) * 128],
                        kT[hp:hp + 64, c, kcol0 + idx * 128: kcol0 + (idx + 1) * 128],
                        qT[hp:hp + 64, c, qcol:qcol + 128], start=True, stop=False),
                        reads=[kres, qres], writes=[("pss", sset)])
                    S.op("pe", I("matmul",
                        ps[:, sb * 512 + idx * 128: sb * 512 + (idx + 1) * 128],
                        tab[:, idx * 128:(idx + 1) * 128], ident_bf, start=False, stop=True),
                        reads=[tabres, "identb"], writes=[("pss", sset)])
                S.op("act", I("activation", out=P[:, 0:W], in_=ps[:, sb * 512: sb * 512 + W], func=AF.Exp),
                     reads=[("pss", sset)], writes=[("P", it % 3)])

            def pv():
                ores = ("ps", ob)
                for idx in range(nj):
                    S.op("pe", I("matmul",
                        bank(ob, 0, 128), vt[:, vblk0 + idx, c * 128:(c + 1) * 128],
                        P[:, idx * 128:(idx + 1) * 128], start=(idx == 0), stop=(idx == nj - 1)),
                        reads=[vres, ("P", it % 3)], writes=[ores])
                for idx in range(nj):
                    S.op("pe", I("matmul",
                        bank(ob, 128, 128), ones_bf, P[:, idx * 128:(idx + 1) * 128],
                        start=(idx == 0), stop=(idx == nj - 1)),
                        reads=["ones", ("P", it % 3)], writes=[ores])
                S.op("dve", I("reciprocal", rden[hp:hp + 64, :], bank(ob, 128, 128)[hp:hp + 64, :]),
                     reads=[ores], writes=[("rden", it % 2)])
                S.op("dve", I("tensor_tensor",
                    ast[hp:hp + 64, c, acol:acol + 128], bank(ob, 0, 128)[hp:hp + 64, :], rden[hp:hp + 64, :],
                    ALU.mult), reads=[ores, ("rden", it % 2)], writes=[astres])
            return qk, pv

        def run_items(items):
            for i, (qk, pv) in enumerate(items):
                qk()
                if i >= 2:
                    items[i - 2][1]()
            for (qk, pv) in items[max(0, len(items) - 2):]:
                pv()

        def phase_B1(l, row_ranges):
            ar.reset()
            TB = 8
            NMAX = TB * GW
            ebint = ar.bf16(NH * 5 * 128).rearrange("p (h w) -> p h w", h=NH)
            load_table(l, rpb_int, 5, ebint, "ebint")
            pw32 = ar.f32(4 * 2 * 256)
            pwb = ar.bf16(4 * 2 * 256).rearrange("p (g k d) -> p g k d", g=4, k=2)
            S.op("sp", I("dma_start", out=pw32, in_=poolw_d[:, l * 2048:(l + 1) * 2048]), writes=["pw32"], chan="pw32")
            S.op("dve", I("tensor_copy", pwb.rearrange("p g k d -> p (g k d)"), pw32), reads=["pw32"], writes=["pwb"])
            zsl = [ar.f32(8 * (NMAX + 16)).rearrange("p (c n) -> p c n", c=8) for _ in range(2)]
            T1 = ar.f32(2 * (NMAX + 16)).rearrange("p (c n) -> p c n", c=2)
            T2 = ar.f32(2 * (NMAX + 16)).rearrange("p (c n) -> p c n", c=2)
            T3 = ar.f32(2 * 8).rearrange("p (c n) -> p c n", c=2)
            pooled = ar.bf16(8 * NMAX).rearrange("p (c n) -> p c n", c=8)
            pmst = [ar.bf16(NMAX) for _ in range(2)]
            qsl = [ar.bf16(8 * NMAX).rearrange("p (c n) -> p c n", c=8) for _ in range(2)]
            ksl = [ar.bf16(8 * (NMAX + 512)).rearrange("p (c n) -> p c n", c=8) for _ in range(2)]
            vsl = [ar.bf16((TB // 2 + 4) * 1024).rearrange("p (b f) -> p b f", f=1024) for _ in range(2)]
            Ps = [ar.bf16(5 * 128) for _ in range(3)]
            rdens = [ar.f32(128) for _ in range(2)]
            asl = [ar.bf16(8 * NMAX).rearrange("p (c n) -> p c n", c=8) for _ in range(2)]

            tiles = []
            for si, (seg, (ra, rb)) in enumerate(zip(geo.segs, row_ranges)):
                Hs = HALO
                r = ra
                while r < rb:
                    nxt = min(rb, ((r - Hs) // TB + 1) * TB + Hs)
                    tiles.append((si, seg, r, nxt - r))
                    r = nxt
            zsrc, qsrc, ksrc, vsrc = [], [], [], []
            for (si, seg, r0, nr) in tiles:
                t0 = seg["off"] + r0 * GW
                n = nr * GW
                zsrc.append(([((lambda s, n=n: s[:, :, 0:n + 16]),
                               ZT[:, :, t0 - 8:t0 + n + 8].rearrange("c p n -> p c n"))],))
                qsrc.append(([((lambda s, n=n: s[:, :, 0:n]), QT[:, :, t0:t0 + n].rearrange("c p n -> p c n"))],))
                ksrc.append(([((lambda s, n=n: s[:, :, 0:n + 512]),
                               KT[:, :, t0 - 256:t0 + n + 256].rearrange("c p n -> p c n"))],))
                vsrc.append(([((lambda s, nr=nr: s[:, 0:nr // 2 + 4, :]),
                               VV[t0 - 256:t0 + n + 256, :].rearrange("(b p) f -> p b f", p=128))],))
            zst = Stream(S, "b1z", zsl, zsrc, 1)
            qst = Stream(S, "b1q", qsl, qsrc, 1)
            kst = Stream(S, "b1k", ksl, ksrc, 1)
            vst = Stream(S, "b1v", vsl, vsrc, 1)
            it = 0
            pmn = 0
            for ti, (si, seg, r0, nr) in enumerate(tiles):
                t0 = seg["off"] + r0 * GW
                n = nr * GW
                zT, zres = zst.get(ti)
                qT, qres = qst.get(ti)
                kT, kres = kst.get(ti)
                vt, vres = vst.get(ti)
                ast = asl[ti % 2]
                astres = ("ast", ti % 2)
                items = []
                for pr in range(nr // 2):
                    for h in range(NH):
                        items.append(attn_item(l, qT, qres, pr * 128, kT, kres, pr * 128, vt, vres, pr, 5, h,
                                               ebint[:, h, :], "ebint",
                                               Ps, rdens, ast, astres, pr * 128, it))
                        it += 1
                run_items(items)
                S.op("sp", I("dma_start",
                    out=AT[:, :, t0:t0 + n].rearrange("c p n -> p c n"), in_=ast[:, :, 0:n]),
                    reads=[astres], chan=f"ast{ti % 2}")
                own0 = seg["off"] + HALO * GW
                own1 = seg["off"] + (HALO + seg["R"]) * GW
                for g in range(4):
                    Z = zT[:, 2 * g:2 * g + 2, :]
                    W = n + 16
                    S.op("pool", I("tensor_tensor", T1[:, :, 1:W], Z[:, :, 0:W - 1], Z[:, :, 1:W], ALU.add),
                         reads=[zres], writes=["T1"])
                    cur, curres = T1, "T1"
                    if g >= 1:
                        S.op("pool", I("tensor_tensor", T2[:, :, 2:W - 1], T1[:, :, 1:W - 2], T1[:, :, 3:W], ALU.add),
                             reads=["T1"], writes=["T2"])
                        cur, curres = T2, "T2"
                    if g >= 2:
                        S.op("pool", I("tensor_tensor", T1[:, :, 4:W - 3], T2[:, :, 2:W - 5], T2[:, :, 6:W - 1], ALU.add),
                             reads=["T2"], writes=["T1"])
                        cur, curres = T1, "T1"
                    if g >= 3:
                        S.op("pool", I("tensor_tensor", T2[:, :, 8:W - 7], T1[:, :, 4:W - 11], T1[:, :, 12:W - 3], ALU.add),
                             reads=["T1"], writes=["T2"])
                        cur, curres = T2, "T2"
                    S.op("dve", I("scalar_tensor_tensor",
                        pooled[:, 2 * g:2 * g + 2, 0:n], cur[:, :, 8:8 + n], 1.0 / POOL_W[g], Z[:, :, 8:8 + n],
                        ALU.mult, ALU.subtract), reads=[curres, zres], writes=["pooled"])
                    for edge, tk in ((0, own0), (1, own1 - 8)):
                        if t0 <= tk and tk + 8 <= t0 + n:
                            u = tk - t0
                            ic = icorr_t[:, ((si * 2 + edge) * 4 + g) * 16:((si * 2 + edge) * 4 + g) * 16 + 16] \
                                .rearrange("p (c n) -> p c n", c=2)
                            S.op("pool", I("tensor_tensor",
                                T3[:, :, :], cur[:, :, 8 + u:16 + u], ic, ALU.mult),
                                reads=[curres, "icorr"], writes=["T3"])
                            S.op("pool", I("tensor_tensor",
                                pooled[:, 2 * g:2 * g + 2, u:u + 8], T3[:, :, :], Z[:, :, 8 + u:16 + u], ALU.subtract),
                                reads=["T3", zres], writes=["pooled"])
                for g in range(4):
                    for m in range(2):
                        b = 6 + (pmn % 2)
                        k = pmn % 2
                        pmn += 1
                        for kc in range(2):
                            S.op("pe", I("matmul",
                                bank(b, 0, n), pwb[:, g, kc, m * 128:(m + 1) * 128], pooled[:, 2 * g + kc, 0:n],
                                start=(kc == 0), stop=(kc == 1)), reads=["pwb", "pooled"], writes=[("ps", b)])
                        S.op("act", I("activation",
                            out=pmst[k][:, 0:n], in_=bank(b, 0, n), func=AF.Identity,
                            scale=pscale[:, l * 8 + 2 * g + m:l * 8 + 2 * g + m + 1]),
                            reads=[("ps", b), "pscale"], writes=[("pmst", k)])
                        S.op("sp", I("dma_start",
                            out=PM[2 * g + m, :, t0:t0 + n], in_=pmst[k][:, 0:n]),
                            reads=[("pmst", k)], chan=f"pmst{k}")
            S.barrier()

        def phase_B1s(l):
            ar.reset()
            ebf = ar.bf16(NH * 7 * 128).rearrange("p (h w) -> p h w", h=NH)
            load_table(l, rpb_exp, 7, ebf, "ebf")
            qsl = [ar.bf16(8 * 128).rearrange("p (c n) -> p c n", c=8) for _ in range(2)]
            ksl = [ar.bf16(8 * 896).rearrange("p (c n) -> p c n", c=8) for _ in range(2)]
            vsl = [ar.bf16(7 * 1024).rearrange("p (b f) -> p b f", f=1024) for _ in range(2)]
            msl = [ar.f32(7 * 128) for _ in range(2)]
            tabs = [ar.bf16(7 * 128) for _ in range(3)]
            Ps = [ar.bf16(7 * 128) for _ in range(3)]
            rdens = [ar.f32(128) for _ in range(2)]
            asl = [ar.bf16(8 * 128).rearrange("p (c n) -> p c n", c=8) for _ in range(2)]
            it = 0
            n_i = 0
            for si, seg in enumerate(geo.segs):
                R = seg["R"]
                for sp, pr in enumerate((HALO, HALO + 2, HALO + R - 4, HALO + R - 2)):
                    k = n_i % 2
                    n_i += 1
                    t0 = seg["off"] + pr * GW
                    S.op("sp", I("dma_start",
                        out=qsl[k], in_=QT[:, :, t0:t0 + 128].rearrange("c p n -> p c n")),
                        writes=[("sq", k)], chan=f"sq{k}")
                    S.op("sp", I("dma_start",
                        out=ksl[k], in_=KT[:, :, t0 - 384:t0 + 512].rearrange("c p n -> p c n")),
                        writes=[("sk", k)], chan=f"sk{k}")
                    S.op("sp", I("dma_start",
                        out=vsl[k], in_=VV[t0 - 384:t0 + 512, :].rearrange("(b p) f -> p b f", p=128)),
                        writes=[("sv", k)], chan=f"sv{k}")
                    S.op("sp", I("dma_start", out=msl[k], in_=amask[si, sp]),
                         writes=[("sm", k)], chan=f"sm{k}")
                    items = []
                    for h in range(NH):
                        tk = it % 3
                        def pre(tk=tk, h=h, k=k):
                            S.op("pool", I("tensor_tensor", tabs[tk], ebf[:, h, :], msl[k], ALU.add),
                                 reads=["ebf", ("sm", k)], writes=[("stab", tk)])
                        items.append(attn_item(l, qsl[k], ("sq", k), 0, ksl[k], ("sk", k), 0, vsl[k], ("sv", k), 0, 7, h,
                                               tabs[tk], ("stab", tk),
                                               Ps, rdens, asl[k], ("sast", k), 0, it, pre=pre))
                        it += 1
                    run_items(items)
                    S.op("sp", I("dma_start",
                        out=AT[:, :, t0:t0 + 128].rearrange("c p n -> p c n"), in_=asl[k]),
                        reads=[("sast", k)], chan=f"sast{k}")
            S.barrier()

        def phase_B3(l, tok_ranges):
            ar.reset()
            NM = 1024
            usl = ar.bf16(KC * NM).rearrange("p (c n) -> p c n", c=KC)
            psl = ar.bf16(8 * NM).rearrange("p (c n) -> p c n", c=8)
            asl = ar.bf16(8 * NM).rearrange("p (c n) -> p c n", c=8)
            mg = ar.bf16(KC * NM).rearrange("p (c n) -> p c n", c=KC)
            w16 = [ar.bf16(KC * 128).rearrange("p (k c) -> p k c", k=KC) for _ in range(5)]
            w8 = [ar.bf16(8 * 128).rearrange("p (k c) -> p k c", k=8) for _ in range(4)]
            ta = [ar.f32(512) for _ in range(2)]
            tb_ = [ar.f32(512) for _ in range(2)]
            xsl = [ar.f32(NM) for _ in range(3)]
            tiles = []
            for seg, (a, b) in zip(geo.segs, tok_ranges):
                t = seg["off"] + a
                end = seg["off"] + b
                while t < end:
                    n = min(NM, end - t)
                    tiles.append((t, n))
                    t += n
            s16, s8, sx = [], [], []
            for (t0, n) in tiles:
                for oc in range(16):
                    s16.append(([((lambda s: s), WG[l, oc])], cdep(f"WG{l}")))
                    s16.append(([((lambda s: s), WG[l, 16 + oc])], cdep(f"WG{l}")))
                    s8.append(([((lambda s: s), WPB[l, oc])], cdep(f"WPB{l}")))
                    s8.append(([((lambda s: s), WAB[l, oc])], cdep(f"WAB{l}")))
                for oc in range(16):
                    s16.append(([((lambda s: s), WO[l, oc])], cdep(f"WO{l}")))
                    sx.append(([((lambda s, n=n: s[:, 0:n]), XT0[oc, :, t0:t0 + n])],))
            st16 = Stream(S, "cw", w16, s16, 3)
            st8 = Stream(S, "c8", w8, s8, 2)
            stx = Stream(S, "cx", xsl, sx, 1)
            i16 = i8 = ix = 0
            ev = 0
            for (t0, n) in tiles:
                chunks = subchunks(n)
                S.op("sp", I("dma_start",
                    out=usl[:, :, 0:n], in_=UT[:, :, t0:t0 + n].rearrange("c p n -> p c n")), writes=["usl"], chan="usl")
                S.op("sp", I("dma_start",
                    out=psl[:, :, 0:n], in_=PM[:, :, t0:t0 + n].rearrange("c p n -> p c n")), writes=["psl"], chan="psl")
                S.op("sp", I("dma_start",
                    out=asl[:, :, 0:n], in_=AT[:, :, t0:t0 + n].rearrange("c p n -> p c n")), writes=["asl"], chan="asl")
                for oc in range(16):
                    wgp, rgp = st16.get(i16)
                    wga, rga = st16.get(i16 + 1, oldest=i16)
                    i16 += 2
                    wpb, rpb_ = st8.get(i8)
                    wab, rab = st8.get(i8 + 1, oldest=i8)
                    i8 += 2
                    for (a, m) in chunks:
                        k = ev % 2
                        b0 = 4 * k
                        ev += 1
                        for kc in range(KC):
                            S.op("pe", I("matmul",
                                bank(b0, 0, m), wgp[:, kc, :], usl[:, kc, a:a + m], start=(kc == 0), stop=(kc == KC - 1)),
                                reads=[rgp, "usl"], writes=[("ps", b0)])
                        for kc in range(8):
                            S.op("pe", I("matmul",
                                bank(b0 + 1, 0, m), wpb[:, kc, :], psl[:, kc, a:a + m], start=(kc == 0), stop=(kc == 7)),
                                reads=[rpb_, "psl"], writes=[("ps", b0 + 1)])
                        for kc in range(KC):
                            S.op("pe", I("matmul",
                                bank(b0 + 2, 0, m), wga[:, kc, :], usl[:, kc, a:a + m], start=(kc == 0), stop=(kc == KC - 1)),
                                reads=[rga, "usl"], writes=[("ps", b0 + 2)])
                        for kc in range(8):
                            S.op("pe", I("matmul",
                                bank(b0 + 3, 0, m), wab[:, kc, :], asl[:, kc, a:a + m], start=(kc == 0), stop=(kc == 7)),
                                reads=[rab, "asl"], writes=[("ps", b0 + 3)])
                        A_, B_ = ta[k], tb_[k]
                        S.op("act", I("activation", out=A_[:, 0:m], in_=bank(b0, 0, m), func=AF.Sigmoid),
                             reads=[("ps", b0)], writes=[("ta", k)])
                        S.op("dve", I("tensor_tensor", A_[:, 0:m], A_[:, 0:m], bank(b0 + 1, 0, m), ALU.mult),
                             reads=[("ta", k), ("ps", b0 + 1)], writes=[("ta", k)])
                        S.op("act", I("activation", out=B_[:, 0:m], in_=bank(b0 + 2, 0, m), func=AF.Sigmoid),
                             reads=[("ps", b0 + 2)], writes=[("tb", k)])
                        S.op("dve", I("tensor_tensor", B_[:, 0:m], B_[:, 0:m], bank(b0 + 3, 0, m), ALU.mult),
                             reads=[("tb", k), ("ps", b0 + 3)], writes=[("tb", k)])
                        S.op("pool", I("tensor_tensor",
                            mg[:, oc, a:a + m], A_[:, 0:m], B_[:, 0:m], ALU.add),
                            reads=[("ta", k), ("tb", k)], writes=["mg"])
                for oc in range(16):
                    wo, rwo = st16.get(i16)
                    i16 += 1
                    xs, rxs = stx.get(ix)
                    kx = ix % 3
                    ix += 1
                    for (a, m) in chunks:
                        b = (ev % 2) * 4
                        ev += 1
                        for kc in range(KC):
                            S.op("pe", I("matmul",
                                bank(b, 0, m), wo[:, kc, :], mg[:, kc, a:a + m], start=(kc == 0), stop=(kc == KC - 1)),
                                reads=[rwo, "mg"], writes=[("ps", b)])
                        S.op("dve", I("tensor_tensor",
                            xs[:, a:a + m], xs[:, a:a + m], bank(b, 0, m), ALU.add),
                            reads=[rxs, ("ps", b)], writes=[rxs])
                    S.op("sp", I("dma_start", out=XM[oc, :, t0:t0 + n], in_=xs[:, 0:n]),
                         reads=[rxs], chan=f"cxs{kx}")
            S.barrier()

        def phase_C(l, tok_ranges, final):
            ar.reset()
            NM = 1022
            xm = ar.f32(KC * NM).rearrange("p (c n) -> p c n", c=KC)
            hT = ar.bf16(KC * NM).rearrange("p (c n) -> p c n", c=KC)
            gT = ar.bf16(GC * 1020).rearrange("p (c n) -> p c n", c=GC)
            sqs = [ar.bf16(1024) for _ in range(3)]
            rstd = ar.f32(1024)
            vmask = ar.f32(1024)
            wus = [ar.bf16(KC * 128).rearrange("p (k c) -> p k c", k=KC) for _ in range(4)]
            wds = [ar.bf16(GC * 128).rearrange("p (k c) -> p k c", k=GC) for _ in range(3)]
            cgs = [ar.f32(512) for _ in range(2)]
            cvs = [ar.f32(512) for _ in range(2)]
            tiles = []
            for si, (seg, (a, b)) in enumerate(zip(geo.segs, tok_ranges)):
                ch = even_chunks(b - a, 510)
                i = 0
                while i < len(ch):
                    grp = ch[i:i + 2]
                    t0 = seg["off"] + a + grp[0][0]
                    tiles.append((si, seg, t0, [(c[0] - grp[0][0], c[1]) for c in grp]))
                    i += 2
            su, sd = [], []
            for _ in tiles:
                for gi in range(NG):
                    for cp in range(GC):
                        i = gi * GC + cp
                        su.append(([((lambda s: s), WUP[l, i])], cdep(f"WUP{l}")))
                        su.append(([((lambda s: s), WUP[l, FC + i])], cdep(f"WUP{l}")))
                    for oc in range(16):
                        sd.append(([((lambda s: s), WDN[l, gi, oc])], cdep(f"WDN{l}")))
            stu = Stream(S, "fu", wus, su, 2)
            std = Stream(S, "fd", wds, sd, 2)
            iu = idn = 0
            ev = 0
            gain = g2[:, l * KC:(l + 1) * KC]
            for (si, seg, t0, chs) in tiles:
                n = sum(c[1] for c in chs)
                S.op("sp", I("dma_start",
                    out=xm[:, :, 0:n + 2], in_=XM[:, :, t0 - 1:t0 + n + 1].rearrange("c p n -> p c n")),
                    writes=["xm"], chan="xm")
                rmsnorm(xm, "xm", n + 2, t0 - 1, gain, sqs, rstd, vmask, 0,
                        lambda c, n=n: hT[:, c, 0:n + 2], "hT", "C")
                for gi in range(NG):
                    for cp in range(GC):
                        i = gi * GC + cp
                        wg, rwg = stu.get(iu)
                        wv_, rwv = stu.get(iu + 1, oldest=iu)
                        iu += 2
                        for (o, m) in chs:
                            k = ev % 2
                            ev += 1
                            bg, bv = 2 + 2 * k, 3 + 2 * k
                            for kc in range(KC):
                                S.op("pe", I("matmul",
                                    bank(bg, 0, m + 2), wg[:, kc, :], hT[:, kc, o:o + m + 2], start=(kc == 0), stop=(kc == KC - 1)),
                                    reads=[rwg, "hT"], writes=[("ps", bg)])
                            for kc in range(KC):
                                S.op("pe", I("matmul",
                                    bank(bv, 0, m + 2), wv_[:, kc, :], hT[:, kc, o:o + m + 2], start=(kc == 0), stop=(kc == KC - 1)),
                                    reads=[rwv, "hT"], writes=[("ps", bv)])
                            for (b, buf, bres, fi) in ((bg, cgs[k], ("cg", k), i), (bv, cvs[k], ("cv", k), FC + i)):
                                cw = lambda j, fi=fi: convw[:, (l * 3 + j) * 88 + fi:(l * 3 + j) * 88 + fi + 1]
                                cb = convb[:, l * 88 + fi:l * 88 + fi + 1]
                                S.op("act", I("activation",
                                    out=buf[:, 0:m], in_=bank(b, 1, m), func=AF.Identity, scale=cw(1), bias=cb),
                                    reads=[("ps", b), "convw", "convb"], writes=[bres])
                                S.op("dve", I("scalar_tensor_tensor",
                                    buf[:, 0:m], bank(b, 0, m), cw(0), buf[:, 0:m], ALU.mult, ALU.add),
                                    reads=[("ps", b), bres, "convw"], writes=[bres])
                                S.op("dve", I("scalar_tensor_tensor",
                                    buf[:, 0:m], bank(b, 2, m), cw(2), buf[:, 0:m], ALU.mult, ALU.add),
                                    reads=[("ps", b), bres, "convw"], writes=[bres])
                            S.op("act", I("activation", out=cgs[k][:, 0:m], in_=cgs[k][:, 0:m], func=AF.Gelu),
                                 reads=[("cg", k)], writes=[("cg", k)])
                            S.op("pool", I("tensor_tensor",
                                gT[:, cp, o:o + m], cgs[k][:, 0:m], cvs[k][:, 0:m], ALU.mult),
                                reads=[("cg", k), ("cv", k)], writes=["gT"])
                    for oc in range(16):
                        wd, rwd = std.get(idn)
                        idn += 1
                        for (o, m) in chs:
                            b = ev % 2
                            ev += 1
                            for cp in range(GC):
                                S.op("pe", I("matmul",
                                    bank(b, 0, m), wd[:, cp, :], gT[:, cp, o:o + m], start=(cp == 0), stop=(cp == GC - 1)),
                                    reads=[rwd, "gT"], writes=[("ps", b)])
                            S.op("dve", I("tensor_tensor",
                                xm[:, oc, 1 + o:1 + o + m], xm[:, oc, 1 + o:1 + o + m], bank(b, 0, m), ALU.add),
                                reads=["xm", ("ps", b)], writes=["xm"])
                if not final:
                    S.op("sp", I("dma_start",
                        out=XT0[:, :, t0:t0 + n].rearrange("c p n -> p c n"), in_=xm[:, :, 1:n + 1]),
                        reads=["xm"], chan="xmst")
                else:
                    xv = xm[:, :, 1:n + 1]
                    rmsnorm(xv, "xm", n, t0, gf, sqs, rstd, vmask, 0,
                            lambda c, n=n: xm[:, c, 1:n + 1], "xm", "F", masked=False)
                    ysl = [hT.rearrange("p c n -> p (c n)")[:, kk * 4096:(kk + 1) * 4096].bitcast(F32) for kk in range(2)]
                    orow = t0 - (seg["off"] + HALO * GW)
                    nb = -(-n // 128)
                    for tb in range(nb):
                        nt = min(128, n - tb * 128)
                        kk = tb % 2
                        for c4 in range(4):
                            b = 2 + (ev % 2)
                            ev += 1
                            for j in range(4):
                                c = c4 * 4 + j
                                S.op("pe", I("transpose",
                                    bank(b, j * 128, 128)[0:nt, :], xm[:, c, 1 + tb * 128:1 + tb * 128 + nt], ident),
                                    reads=["xm", "ident"], writes=[("ps", b)])
                            if c4 % 2 == 0:
                                S.op("act", I("activation",
                                    out=ysl[kk][0:nt, c4 * 512:(c4 + 1) * 512], in_=bank(b, 0, 512)[0:nt, :], func=AF.Copy),
                                    reads=[("ps", b)], writes=[("ysl", kk)])
                            else:
                                S.op("dve", I("tensor_copy",
                                    ysl[kk][0:nt, c4 * 512:(c4 + 1) * 512], bank(b, 0, 512)[0:nt, :]),
                                    reads=[("ps", b)], writes=[("ysl", kk)])
                        S.op("sp", I("dma_start",
                            out=youts[si][orow + tb * 128:orow + tb * 128 + nt, :], in_=ysl[kk][0:nt, :]),
                            reads=[("ysl", kk), "hT"], chan=f"ysl{kk}")
            S.barrier()

        H = HALO
        segs = geo.segs
        steps = [
            lambda: phase_A(0, [(0, s["E"]) for s in segs]),
            lambda: phase_B1(0, [(H - 8, H + s["R"] + 8) for s in segs]),
            lambda: phase_B1s(0),
            lambda: phase_B3(0, [((H - 6) * GW - 8, (H + s["R"] + 6) * GW + 8) for s in segs]),
            lambda: phase_C(0, [((H - 6) * GW, (H + s["R"] + 6) * GW) for s in segs], final=False),
            lambda: phase_A(1, [(H - 6, H + s["R"] + 6) for s in segs]),
            lambda: phase_B1(1, [(H - 2, H + s["R"] + 2) for s in segs]),
            lambda: phase_B1s(1),
            lambda: phase_B3(1, [(H * GW - 8, (H + s["R"]) * GW + 8) for s in segs]),
            lambda: phase_C(1, [(H * GW, (H + s["R"]) * GW) for s in segs], final=True),
        ]
        for st in steps[:stop_after]:
            st()
        S.emit()
    return nc, geo


def _col_valid():
    qc = np.arange(GW)
    cs = np.clip(qc - 8, 0, GW - 16)
    kc = np.arange(GW)
    return (kc[:, None] >= cs[None, :]) & (kc[:, None] < cs[None, :] + 16)


def _bias_tables(rpb, js, dr_lo, dr_hi):
    Ln, Hn = rpb.shape[0], rpb.shape[1]
    cv = _col_valid()
    kc = np.arange(GW)[:, None]
    qc = np.arange(GW)[None, :]
    dci = np.clip(kc - qc + 15, 0, 30)
    out = np.full((Ln, Hn, 128, len(js), 128), NEG, dtype=np.float32)
    for ji, j in enumerate(js):
        for kp in range(2):
            for qp in range(2):
                dr = 2 * j + kp - qp
                if dr < dr_lo or dr > dr_hi:
                    continue
                blk = rpb[:, :, dr + 7, :][:, :, dci]
                blk = np.where(cv[None, None], blk, np.float32(NEG))
                out[:, :, kp * 64:(kp + 1) * 64, ji, qp * 64:(qp + 1) * 64] = blk
    return np.ascontiguousarray(out.transpose(0, 1, 4, 3, 2)).reshape(Ln, Hn, 128, len(js) * 128)


def _core_layout(ci, RP, RS, rows_p, rows_s, x_prompt, x_sample):
    geo = Geo(RP, RS)
    npq = rows_p // RP
    nsq = rows_s // RS
    info = [(x_prompt[ci // npq], (ci % npq) * RP, rows_p), (x_sample[ci // nsq], (ci % nsq) * RS, rows_s)]
    x_ext = np.zeros((geo.NT, D), np.float32)
    valid = np.zeros((1, geo.NT), np.float32)
    icorr = np.zeros((2, 2, 4, 2, 8), np.float32)
    amask = np.zeros((2, 4, 128, 7, 128), np.float32)
    for si, (seg, (xs, row0, rows)) in enumerate(zip(geo.segs, info)):
        R, E, off = seg["R"], seg["E"], seg["off"]
        g0 = row0 - HALO
        lo = max(0, g0)
        hi = min(rows, g0 + E)
        x_ext[off + (lo - g0) * GW: off + (hi - g0) * GW] = xs[lo * GW:hi * GW]
        valid[0, off + (lo - g0) * GW: off + (hi - g0) * GW] = 1.0
        nseq = rows * GW
        for edge, tk in ((0, row0 * GW), (1, (row0 + R) * GW - 8)):
            for g, w in enumerate(POOL_W):
                t = tk + np.arange(8)
                lo_t = np.clip(t - w // 2, 0, nseq - 1)
                hi_t = np.clip(t + (w - w // 2) - 1, 0, nseq - 1)
                icorr[si, edge, g, :, :] = (1.0 / (hi_t - lo_t + 1).astype(np.float32))[None, :]
        for sp, pr in enumerate((0, 2, R - 4, R - 2)):
            for ji, j in enumerate(range(-3, 4)):
                for kp in range(2):
                    for qp in range(2):
                        r = row0 + pr + qp
                        rho = row0 + pr + 2 * j + kp
                        s = min(max(r - 4, 0), rows - 8)
                        ok = (s <= rho <= s + 7)
                        if not ok:
                            amask[si, sp, qp * 64:(qp + 1) * 64, ji, kp * 64:(kp + 1) * 64] = NEG
    icorr_b = np.ascontiguousarray(np.broadcast_to(icorr.reshape(1, -1), (128, icorr.size)))
    return dict(x_ext=x_ext, valid=valid, icorr=icorr_b, amask=amask.reshape(2, 4, 128, 7 * 128))


def _run(x_prompt, x_sample, norm1_g, w_in, pool_w, pool_scale, rpb, w_pool_br, w_attn_br, w_out,
         norm2_g, w_up, conv_w, conv_b, w_down, norm_f, debug=False):
    f = lambda a: np.ascontiguousarray(np.asarray(a, dtype=np.float32))
    x_prompt, x_sample = f(x_prompt), f(x_sample)
    rows_p = x_prompt.shape[1] // GW
    rows_s = x_sample.shape[1] // GW
    nb_p, nb_s = x_prompt.shape[0], x_sample.shape[0]
    RP = rows_p * nb_p // 8
    RS = rows_s * nb_s // 8
    nc, geo = build_program(RP, RS, debug=debug)
    rpb = f(rpb)
    common = dict(
        rpb_exp=_bias_tables(rpb, list(range(-3, 4)), -7, 7),
        rpb_int=_bias_tables(rpb, list(range(-2, 3)), -4, 3),
        ident=np.eye(128, dtype=np.float32),
        w_in=f(w_in), w_pool_br=f(w_pool_br), w_attn_br=f(w_attn_br), w_out=f(w_out), w_up=f(w_up), w_down=f(w_down),
        poolw=np.ascontiguousarray(f(pool_w).reshape(NL, 4, 2, 128, 256).transpose(3, 0, 1, 2, 4).reshape(128, -1)),
        g1=np.ascontiguousarray(f(norm1_g).reshape(NL, KC, 128).transpose(2, 0, 1).reshape(128, -1)),
        g2=np.ascontiguousarray(f(norm2_g).reshape(NL, KC, 128).transpose(2, 0, 1).reshape(128, -1)),
        gf=np.ascontiguousarray(f(norm_f).reshape(KC, 128).T),
        pscale=np.ascontiguousarray(f(pool_scale).reshape(NL, 8, 128).transpose(2, 0, 1).reshape(128, -1)),
        convw=np.ascontiguousarray(f(conv_w).reshape(NL, 3, 88, 128).transpose(3, 0, 1, 2).reshape(128, -1)),
        convb=np.ascontiguousarray(f(conv_b).reshape(NL, 88, 128).transpose(2, 0, 1).reshape(128, -1)),
    )
    in_maps = []
    for ci in range(8):
        m = dict(common)
        m.update(_core_layout(ci, RP, RS, rows_p, rows_s, x_prompt, x_sample))
        in_maps.append(m)
    res = run_bass_kernel_spmd(nc, in_maps, core_ids=list(range(8)))
    y_p = np.empty_like(x_prompt)
    y_s = np.empty_like(x_sample)
    npq = rows_p // RP
    nsq = rows_s // RS
    for ci in range(8):
        r = res.results[ci]
        y_p[ci // npq, (ci % npq) * RP * GW:((ci % npq) + 1) * RP * GW] = r["y_p"]
        y_s[ci // nsq, (ci % nsq) * RS * GW:((ci % nsq) + 1) * RS * GW] = r["y_s"]
    if debug:
        return (y_p, y_s), res, geo
    return (y_p, y_s)


def kernel(x_prompt, x_sample, norm1_g, w_in, pool_w, pool_scale, rpb, w_pool_br, w_attn_br, w_out,
           norm2_g, w_up, conv_w, conv_b, w_down, norm_f):
    return _run(x_prompt, x_sample, norm1_g, w_in, pool_w, pool_scale, rpb, w_pool_br, w_attn_br, w_out,
                norm2_g, w_up, conv_w, conv_b, w_down, norm_f)
```

```python
import numpy as np
from contextlib import ExitStack
import concourse.bass as bass
import concourse.mybir as mybir
from concourse.bass_utils import run_bass_kernel_spmd

F32 = mybir.dt.float32
BF16 = mybir.dt.bfloat16
AF = mybir.ActivationFunctionType
ALU = mybir.AluOpType
AX = mybir.AxisListType

D = 2048
KC = 16
NL = 2
GW = 64
NH = 16
FF = 5632
FC = 44
NG = 4
GC = 11
HALO = 12
EPS = 1e-6
NEG = -30000.0
POOL_W = (2, 4, 8, 16)


def I(method, *args, **kwargs):
    return lambda e: getattr(e, method)(*args, **kwargs)


class Op:
    __slots__ = ("eng", "fn", "deps", "chan", "token", "has_dep", "inc")

    def __init__(self, eng, fn, chan):
        self.eng = eng
        self.fn = fn
        self.chan = chan
        self.deps = {}
        self.token = None
        self.has_dep = False
        self.inc = False


class Sched:
    ENGS = ("pe", "act", "dve", "pool", "sp")

    def __init__(self, nc, es):
        self.nc = nc
        self.es = es
        self.ops = {e: [] for e in self.ENGS}
        self.last_write = {}
        self.readers = {}
        self.chan_cnt = {}
        self.chan_last = {}
        self.sems = {}
        self.bg_chans = set()

    def sem(self, name):
        if name not in self.sems:
            self.sems[name] = self.es.enter_context(self.nc.semaphore("s_" + name))
        return self.sems[name]

    @staticmethod
    def _key(op):
        return op.chan if op.chan is not None else op.eng

    def _add_dep(self, o, d):
        if d is None or d is o:
            return
        k = self._key(d)
        o.deps[(k, id(d))] = d

    def op(self, eng, fn, reads=(), writes=(), chan=None, extra=()):
        o = Op(eng, fn, chan)
        for r in reads:
            self._add_dep(o, self.last_write.get(r))
        for w in writes:
            self._add_dep(o, self.last_write.get(w))
            for d in self.readers.get(w, {}).values():
                self._add_dep(o, d)
        for d in extra:
            self._add_dep(o, d)
        for d in o.deps.values():
            d.has_dep = True
        for r in reads:
            self.readers.setdefault(r, {})[self._key(o)] = o
        for w in writes:
            self.last_write[w] = o
            self.readers[w] = {}
        if chan is not None:
            n = self.chan_cnt.get(chan, 0) + 1
            self.chan_cnt[chan] = n
            o.token = ("c_" + chan, 16 * n)
            self.chan_last[chan] = o
        self.ops[eng].append(o)
        return o

    def barrier(self):
        lasts = []
        for e in self.ENGS:
            for o in reversed(self.ops[e]):
                if o.fn is not None and o.chan is None:
                    lasts.append(o)
                    break
        for c, o in self.chan_last.items():
            if c not in self.bg_chans:
                lasts.append(o)
        for e in self.ENGS:
            b = Op(e, None, None)
            for d in lasts:
                if d.eng == e and d.chan is None:
                    continue
                self._add_dep(b, d)
                d.has_dep = True
            self.ops[e].append(b)
        self.last_write = {}
        self.readers = {}

    def check(self):
        sem = {}
        pos = {e: 0 for e in self.ENGS}
        progress = True
        while progress:
            progress = False
            for e in self.ENGS:
                lst = self.ops[e]
                while pos[e] < len(lst):
                    o = lst[pos[e]]
                    ok = True
                    for d in o.deps.values():
                        if d.chan is None and d.eng == "pe" and e == "pe":
                            continue
                        if d.token is None:
                            continue
                        s, v = d.token
                        if sem.get(s, 0) < v:
                            ok = False
                            break
                    if not ok:
                        break
                    if o.fn is not None:
                        if o.chan is not None:
                            sem["c_" + o.chan] = sem.get("c_" + o.chan, 0) + 16
                        elif o.inc:
                            sem["e_" + e] = sem.get("e_" + e, 0) + 1
                    pos[e] += 1
                    progress = True
        stuck = {e: (pos[e], len(self.ops[e])) for e in self.ENGS if pos[e] < len(self.ops[e])}
        if stuck:
            raise RuntimeError(f"semaphore protocol deadlock: {stuck}")
        self.n_ops = {e: len(self.ops[e]) for e in self.ENGS}

    def emit(self):
        nc = self.nc
        for e in self.ENGS:
            cnt = 0
            for o in self.ops[e]:
                if o.chan is None and o.fn is not None and o.has_dep:
                    cnt += 1
                    o.token = ("e_" + e, cnt)
                    o.inc = True
        self.check()
        for e in self.ENGS:
            self.sem("e_" + e)
        for c in self.chan_cnt:
            self.sem("c_" + c)
        sched = self

        def body_for(ename):
            def body(e):
                waited = {}
                for o in sched.ops[ename]:
                    need = {}
                    for d in o.deps.values():
                        if d.chan is None and d.eng == "pe" and ename == "pe":
                            continue
                        if d.token is None:
                            continue
                        s, v = d.token
                        if need.get(s, 0) < v:
                            need[s] = v
                    for s, v in need.items():
                        if waited.get(s, 0) < v:
                            e.wait_ge(sched.sems[s], v)
                            waited[s] = v
                    if o.fn is not None:
                        ins = o.fn(e)
                        if o.chan is not None:
                            ins.then_inc(sched.sems["c_" + o.chan], 16)
                        elif o.inc:
                            ins.then_inc(sched.sems["e_" + ename], 1)
            return body

        with nc.Block() as block:
            block.tensor(body_for("pe"))
            block.scalar(body_for("act"))
            block.vector(body_for("dve"))
            block.gpsimd(body_for("pool"))
            block.sync(body_for("sp"))


class Arena:
    def __init__(self, ap, words):
        self.ap = ap
        self.words = words
        self.base = 0
        self.pos = 0
        self.uid = 0

    def set_base(self):
        self.base = self.pos

    def reset(self):
        self.pos = self.base

    def f32(self, n):
        a = self.pos
        self.pos += n
        assert self.pos <= self.words, f"SBUF arena overflow {self.pos} > {self.words}"
        return self.ap[:, a:a + n]

    def bf16(self, n):
        w = (n + 1) // 2
        a = self.pos
        self.pos += w
        assert self.pos <= self.words, f"SBUF arena overflow {self.pos} > {self.words}"
        return self.ap[:, a:a + w].bitcast(BF16)[:, 0:n]

    def name(self, s):
        self.uid += 1
        return f"{s}#{self.uid}"


class Stream:
    def __init__(self, sched, name, slots, srcs, depth, eng="sp"):
        self.s = sched
        self.name = name
        self.slots = slots
        self.srcs = srcs
        self.depth = min(depth, len(slots) - 1)
        self.nxt = 0
        self.eng = eng

    def _issue(self, i):
        k = i % len(self.slots)
        slot = self.slots[k]
        spec = self.srcs[i]
        parts = spec[0]
        extra = spec[1] if len(spec) > 1 else ()
        for (dst_fn, src) in parts:
            dst = dst_fn(slot)
            self.s.op(self.eng, I("dma_start", out=dst, in_=src),
                      writes=[(self.name, k)], chan=f"{self.name}{k}", extra=extra)

    def get(self, i, oldest=None):
        oldest = i if oldest is None else oldest
        while self.nxt < len(self.srcs) and self.nxt <= i + self.depth:
            assert self.nxt - len(self.slots) < oldest, "stream slot still live"
            self._issue(self.nxt)
            self.nxt += 1
        k = i % len(self.slots)
        return self.slots[k], (self.name, k)


def subchunks(n, mx=512):
    out = []
    a = 0
    while a < n:
        b = min(n, a + mx)
        out.append((a, b - a))
        a = b
    return out


def even_chunks(n, mx):
    k = -(-n // mx)
    base = n // k
    rem = n - base * k
    out = []
    a = 0
    for i in range(k):
        sz = base + (1 if i < rem else 0)
        out.append((a, sz))
        a += sz
    return out


class Geo:
    def __init__(self, RP, RS):
        self.segs = []
        off = 0
        for nm, R in (("P", RP), ("S", RS)):
            E = R + 2 * HALO
            self.segs.append(dict(name=nm, R=R, E=E, off=off))
            off += E * GW
        self.NT = off


def build_program(RP, RS, debug=False, stop_after=None):
    geo = Geo(RP, RS)
    NT = geo.NT
    nc = bass.Bass("TRN2", target_bir_lowering=False)
    es = ExitStack()

    def din(name, shape, dt=F32):
        return nc.dram_tensor(name, list(shape), dt, kind="ExternalInput").ap()

    def dscr(name, shape, dt):
        if debug:
            return nc.dram_tensor(name, list(shape), dt, kind="ExternalOutput").ap()
        return nc.dram_tensor(name, list(shape), dt).ap()

    x_ext = din("x_ext", [NT, D])
    valid = din("valid", [1, NT])
    icorr = din("icorr", [128, 2 * 2 * 4 * 2 * 8])
    amask = din("amask", [2, 4, 128, 7 * 128])
    rpb_exp = din("rpb_exp", [NL, NH, 128, 7 * 128])
    rpb_int = din("rpb_int", [NL, NH, 128, 5 * 128])
    ident_d = din("ident", [128, 128])
    w_in = din("w_in", [NL, D, 8192])
    w_pb = din("w_pool_br", [NL, 1024, D])
    w_ab = din("w_attn_br", [NL, 1024, D])
    w_out = din("w_out", [NL, D, D])
    w_up = din("w_up", [NL, D, 2 * FF])
    w_dn = din("w_down", [NL, FF, D])
    poolw_d = din("poolw", [128, NL * 4 * 2 * 256])
    g1_d = din("g1", [128, NL * KC])
    g2_d = din("g2", [128, NL * KC])
    gf_d = din("gf", [128, KC])
    pscale_d = din("pscale", [128, NL * 8])
    convw_d = din("convw", [128, NL * 3 * 88])
    convb_d = din("convb", [128, NL * 88])

    y_p = nc.dram_tensor("y_p", [RP * GW, D], F32, kind="ExternalOutput").ap()
    y_s = nc.dram_tensor("y_s", [RS * GW, D], F32, kind="ExternalOutput").ap()
    youts = [y_p, y_s]

    XT0 = dscr("XT0", [KC, 128, NT], F32)
    XM = dscr("XM", [KC, 128, NT], F32)
    UT = dscr("UT", [KC, 128, NT], BF16)
    ZT = dscr("ZT", [8, 128, NT], F32)
    QT = dscr("QT", [8, 128, NT], BF16)
    KT = dscr("KT", [8, 128, NT], BF16)
    VV = dscr("VV", [NT, 1024], BF16)
    PM = dscr("PM", [8, 128, NT], BF16)
    AT = dscr("AT", [8, 128, NT], BF16)
    WA = nc.dram_tensor("WA", [NL, 24, 128, KC, 128], BF16).ap()
    WV = nc.dram_tensor("WV", [NL, 128, KC, 1024], BF16).ap()
    WG = nc.dram_tensor("WG", [NL, 32, 128, KC, 128], BF16).ap()
    WPB = nc.dram_tensor("WPB", [NL, 16, 128, 8, 128], BF16).ap()
    WAB = nc.dram_tensor("WAB", [NL, 16, 128, 8, 128], BF16).ap()
    WO = nc.dram_tensor("WO", [NL, 16, 128, KC, 128], BF16).ap()
    WUP = nc.dram_tensor("WUP", [NL, 88, 128, KC, 128], BF16).ap()
    WDN = nc.dram_tensor("WDN", [NL, NG, 16, 128, GC, 128], BF16).ap()

    with es:
        S = Sched(nc, es)
        AW = 51200
        arena_t = es.enter_context(nc.sbuf_tensor("arena", [128, AW], F32))
        ar = Arena(arena_t[:], AW)
        ps_t = es.enter_context(nc.psum_tensor("ps", [128, 8 * 512], F32))
        ps = ps_t[:]

        def bank(b, a=0, n=512):
            return ps[:, b * 512 + a: b * 512 + a + n]

        ident = ar.f32(128)
        ones_bf = ar.bf16(128)
        g1 = ar.f32(NL * KC)
        g2 = ar.f32(NL * KC)
        gf = ar.f32(KC)
        pscale = ar.f32(NL * 8)
        convw = ar.f32(NL * 3 * 88)
        convb = ar.f32(NL * 88)
        icorr_t = ar.f32(2 * 2 * 4 * 2 * 8)
        ar.set_base()

        def ld_const(dst, src, nm):
            S.op("sp", I("dma_start", out=dst, in_=src), writes=[nm], chan="const")

        ld_const(ident, ident_d[:, :], "ident")
        ld_const(g1, g1_d[:, :], "g1")
        ld_const(g2, g2_d[:, :], "g2")
        ld_const(gf, gf_d[:, :], "gf")
        ld_const(pscale, pscale_d[:, :], "pscale")
        ld_const(convw, convw_d[:, :], "convw")
        ld_const(convb, convb_d[:, :], "convb")
        ld_const(icorr_t, icorr[:, :], "icorr")
        S.op("dve", I("memset", ones_bf, 1.0), writes=["ones"])

        conv_ops = {}

        def conv(key, dst, src):
            S.bg_chans.add("cv_" + key)
            conv_ops[key] = S.op("pool", I("dma_start", out=dst, in_=src), chan="cv_" + key)

        def cols(w, l, c0, n):
            return w[l, :, c0:c0 + n].rearrange("(kc p) c -> p kc c", p=128)

        for l in range(NL):
            for oc in range(24):
                conv(f"WA{l}", WA[l, oc], cols(w_in, l, oc * 128, 128))
            conv(f"WV{l}", WV[l], cols(w_in, l, 3072, 1024))
            for oc in range(32):
                conv(f"WG{l}", WG[l, oc], cols(w_in, l, 4096 + oc * 128, 128))
            for oc in range(16):
                conv(f"WPB{l}", WPB[l, oc], cols(w_pb, l, oc * 128, 128))
                conv(f"WAB{l}", WAB[l, oc], cols(w_ab, l, oc * 128, 128))
            for oc in range(16):
                conv(f"WO{l}", WO[l, oc], cols(w_out, l, oc * 128, 128))
            for i in range(88):
                conv(f"WUP{l}", WUP[l, i], cols(w_up, l, i * 128, 128))
            for gi in range(NG):
                for oc in range(16):
                    conv(f"WDN{l}", WDN[l, gi, oc],
                         w_dn[l, gi * GC * 128:(gi + 1) * GC * 128, oc * 128:(oc + 1) * 128]
                         .rearrange("(cp p) c -> p cp c", p=128))

        def cdep(key):
            return (conv_ops[key],)

        S.barrier()

        def rmsnorm(xT, xres, n, tok0, gain, sqslots, rstd, vmask, psb, out_fn, out_res, tag, masked=True):
            chunks = subchunks(n)
            if masked:
                S.op("sp", I("dma_start", out=vmask[:, 0:n],
                                                  in_=valid[0:1, tok0:tok0 + n].partition_broadcast(128)),
                     writes=[tag + "vm"], chan=tag + "vm")
            for c in range(KC):
                k = c % len(sqslots)
                sq = sqslots[k]
                S.op("act", I("activation", out=sq[:, 0:n], in_=xT[:, c, 0:n], func=AF.Square),
                     reads=[xres], writes=[(tag + "sq", k)])
                for si, (a, m) in enumerate(chunks):
                    S.op("pe", I("matmul",
                        bank(psb + si, 0, m), ones_bf, sq[:, a:a + m], start=(c == 0), stop=(c == KC - 1)),
                        reads=[(tag + "sq", k), "ones"], writes=[("ps", psb + si)])
            for si, (a, m) in enumerate(chunks):
                S.op("act", I("activation",
                    out=rstd[:, a:a + m], in_=bank(psb + si, 0, m), func=AF.Sqrt, scale=1.0 / D, bias=EPS),
                    reads=[("ps", psb + si)], writes=[tag + "rstd"])
            S.op("dve", I("reciprocal", rstd[:, 0:n], rstd[:, 0:n]), reads=[tag + "rstd"], writes=[tag + "rstd"])
            if masked:
                S.op("dve", I("tensor_tensor", rstd[:, 0:n], rstd[:, 0:n], vmask[:, 0:n], ALU.mult),
                     reads=[tag + "rstd", tag + "vm"], writes=[tag + "rstd"])
            for c in range(KC):
                S.op("dve", I("scalar_tensor_tensor",
                    out_fn(c), xT[:, c, 0:n], gain[:, c:c + 1], rstd[:, 0:n], ALU.mult, ALU.mult),
                    reads=[xres, tag + "rstd"], writes=[out_res])

        def phase_A(l, row_ranges):
            ar.reset()
            wv = ar.bf16(KC * 1024).rearrange("p (k f) -> p k f", k=KC)
            xT = ar.f32(KC * 1024).rearrange("p (c n) -> p c n", c=KC)
            uT = ar.bf16(KC * 1024).rearrange("p (c n) -> p c n", c=KC)
            sqs = [ar.bf16(1024) for _ in range(3)]
            rstd = ar.f32(1024)
            vmask = ar.f32(1024)
            wslots = [ar.bf16(KC * 128).rearrange("p (k c) -> p k c", k=KC) for _ in range(4)]
            stz = [ar.f32(1024) for _ in range(2)]
            stb = [ar.bf16(1024) for _ in range(3)]
            stv = [ar.bf16(1024) for _ in range(2)]
            xin = [ar.f32(D) for _ in range(2)] if l == 0 else None
            gain = g1[:, l * KC:(l + 1) * KC]

            S.op("sp", I("dma_start", out=wv, in_=WV[l]), writes=["wv"], chan="wv", extra=cdep(f"WV{l}"))

            tiles = []
            for seg, (ra, rb) in zip(geo.segs, row_ranges):
                r = ra
                while r < rb:
                    nr = min(16, rb - r)
                    tiles.append((seg["off"] + r * GW, nr * GW))
                    r += nr
            srcs = []
            for (t0, n) in tiles:
                for oc in range(24):
                    srcs.append(([((lambda s: s), WA[l, oc])], cdep(f"WA{l}")))
            wst = Stream(S, "aw", wslots, srcs, depth=3)
            widx = 0
            ev = 0
            nst = {"z": 0, "b": 0, "v": 0}
            for ti, (t0, n) in enumerate(tiles):
                chunks = subchunks(n)
                if l == 0:
                    for tb in range(n // 128):
                        k = tb % 2
                        S.op("sp", I("dma_start",
                            out=xin[k], in_=x_ext[t0 + tb * 128:t0 + (tb + 1) * 128, :]),
                            writes=[("xin", k)], chan=f"xin{k}")
                        for c4 in range(4):
                            b = 2 + (c4 % 2)
                            for j in range(4):
                                c = c4 * 4 + j
                                S.op("pe", I("transpose",
                                    bank(b, j * 128, 128), xin[k][:, c * 128:(c + 1) * 128], ident),
                                    reads=[("xin", k), "ident"], writes=[("ps", b)])
                            eng = "act" if (c4 % 2 == 0) else "dve"
                            dst = xT[:, c4 * 4:c4 * 4 + 4, tb * 128:(tb + 1) * 128]
                            src = bank(b, 0, 512).rearrange("p (j t) -> p j t", j=4)
                            if eng == "act":
                                S.op("act", I("activation", out=dst, in_=src, func=AF.Copy),
                                     reads=[("ps", b)], writes=["xT"])
                            else:
                                S.op("dve", I("tensor_copy", dst, src),
                                     reads=[("ps", b)], writes=["xT"])
                    S.op("sp", I("dma_start",
                        out=XT0[:, :, t0:t0 + n].rearrange("c p n -> p c n"), in_=xT[:, :, 0:n]),
                        reads=["xT"], chan="xTst")
                else:
                    S.op("sp", I("dma_start",
                        out=xT[:, :, 0:n], in_=XT0[:, :, t0:t0 + n].rearrange("c p n -> p c n")),
                        writes=["xT"], chan="xTld")
                rmsnorm(xT, "xT", n, t0, gain, sqs, rstd, vmask, 0,
                        lambda c: uT[:, c, 0:n], "uT", "A")
                S.op("sp", I("dma_start",
                    out=UT[:, :, t0:t0 + n].rearrange("c p n -> p c n"), in_=uT[:, :, 0:n]),
                    reads=["uT"], chan="uTst")
                for oc in range(24):
                    w, wres = wst.get(widx)
                    widx += 1
                    if oc < 8:
                        k = nst["z"] % 2
                        nst["z"] += 1
                        stage, sres, chn = stz[k], ("stz", k), f"stz{k}"
                        dst = ZT[oc, :, t0:t0 + n]
                    else:
                        k = nst["b"] % 3
                        nst["b"] += 1
                        stage, sres, chn = stb[k], ("stb", k), f"stb{k}"
                        dst = (QT if oc < 16 else KT)[oc % 8, :, t0:t0 + n]
                    for (a, m) in chunks:
                        b = 4 + (ev % 4)
                        ev += 1
                        for kc in range(KC):
                            S.op("pe", I("matmul",
                                bank(b, 0, m), w[:, kc, :], uT[:, kc, a:a + m], start=(kc == 0), stop=(kc == KC - 1)),
                                reads=[wres, "uT"], writes=[("ps", b)])
                        sc = 0.125 if 8 <= oc < 16 else 1.0
                        if ev % 2 == 0:
                            S.op("act", I("activation",
                                out=stage[:, a:a + m], in_=bank(b, 0, m), func=AF.Copy, scale=sc),
                                reads=[("ps", b)], writes=[sres])
                        else:
                            S.op("dve", I("tensor_scalar",
                                stage[:, a:a + m], bank(b, 0, m), sc, 0.0, ALU.mult, ALU.add),
                                reads=[("ps", b)], writes=[sres])
                    S.op("sp", I("dma_start", out=dst, in_=stage[:, 0:n]),
                         reads=[sres], chan=chn)
                for tb in range(n // 128):
                    k = nst["v"] % 2
                    nst["v"] += 1
                    stage = stv[k]
                    for vc in range(2):
                        b = 4 + (ev % 4)
                        ev += 1
                        for kc in range(KC):
                            S.op("pe", I("matmul",
                                bank(b, 0, 512), uT[:, kc, tb * 128:(tb + 1) * 128], wv[:, kc, vc * 512:(vc + 1) * 512],
                                start=(kc == 0), stop=(kc == KC - 1)),
                                reads=["wv", "uT"], writes=[("ps", b)])
                        if ev % 2 == 0:
                            S.op("act", I("activation",
                                out=stage[:, vc * 512:(vc + 1) * 512], in_=bank(b, 0, 512), func=AF.Copy),
                                reads=[("ps", b)], writes=[("stv", k)])
                        else:
                            S.op("dve", I("tensor_copy",
                                stage[:, vc * 512:(vc + 1) * 512], bank(b, 0, 512)),
                                reads=[("ps", b)], writes=[("stv", k)])
                    S.op("sp", I("dma_start",
                        out=VV[t0 + tb * 128:t0 + (tb + 1) * 128, :], in_=stage[:, 0:1024]),
                        reads=[("stv", k)], chan=f"stv{k}")
            S.barrier()

        def load_table(l, src_d, nj, dst, tag):
            stg = [ar.f32(nj * 128) for _ in range(2)]
            for h in range(NH):
                k = h % 2
                S.op("sp", I("dma_start", out=stg[k], in_=src_d[l, h]),
                     writes=[(tag + "stg", k)], chan=f"{tag}stg{k}")
                S.op("act", I("activation", out=dst[:, h, :], in_=stg[k], func=AF.Exp),
                     reads=[(tag + "stg", k)], writes=[tag])

        def attn_item(l, qT, qres, qcol, kT, kres, kcol0, vt, vres, vblk0, nj, h, tab, tabres, mask, maskres,
                      Es, Ps, rdens, ast, astres, acol, it):
            c = h // 2
            hp = (h % 2) * 64
            sset = it % 3
            sb = 2 * sset
            ob = 6 + (it % 2)
            E = Es[it % 3]
            P = Ps[it % 3]
            rden = rdens[it % 2]
            W = nj * 128
            meng = "pool" if (it % 3 == 2) else "dve"

            def qk():
                for idx in range(nj):
                    S.op("pe", I("matmul",
                        ps[:, sb * 512 + idx * 128: sb * 512 + (idx + 1) * 128],
                        kT[hp:hp + 64, c, kcol0 + idx * 128: kcol0 + (idx + 1) * 128],
                        qT[hp:hp + 64, c, qcol:qcol + 128], start=True, stop=True),
                        reads=[kres, qres], writes=[("pss", sset)])
                S.op("act", I("activation", out=E[:, 0:W], in_=ps[:, sb * 512: sb * 512 + W], func=AF.Exp),
                     reads=[("pss", sset)], writes=[("E", it % 3)])
                S.op(meng, I("tensor_tensor", P[:, 0:W], E[:, 0:W], tab, ALU.mult),
                     reads=[("E", it % 3), tabres], writes=[("P", it % 3)])
                if mask is not None:
                    S.op(meng, I("tensor_tensor", P[:, 0:W], P[:, 0:W], mask, ALU.mult),
                         reads=[("P", it % 3), maskres], writes=[("P", it % 3)])

            def pv():
                ores = ("ps", ob)
                for idx in range(nj):
                    S.op("pe", I("matmul",
                        bank(ob, 0, 128), vt[:, vblk0 + idx, c * 128:(c + 1) * 128],
                        P[:, idx * 128:(idx + 1) * 128], start=(idx == 0), stop=(idx == nj - 1)),
                        reads=[vres, ("P", it % 3)], writes=[ores])
                for idx in range(nj):
                    S.op("pe", I("matmul",
                        bank(ob, 128, 128), ones_bf, P[:, idx * 128:(idx + 1) * 128],
                        start=(idx == 0), stop=(idx == nj - 1)),
                        reads=["ones", ("P", it % 3)], writes=[ores])
                S.op("dve", I("reciprocal", rden[hp:hp + 64, :], bank(ob, 128, 128)[hp:hp + 64, :]),
                     reads=[ores], writes=[("rden", it % 2)])
                S.op("dve", I("tensor_tensor",
                    ast[hp:hp + 64, c, acol:acol + 128], bank(ob, 0, 128)[hp:hp + 64, :], rden[hp:hp + 64, :],
                    ALU.mult), reads=[ores, ("rden", it % 2)], writes=[astres])
            return qk, pv

        def run_items(items):
            for i, (qk, pv) in enumerate(items):
                qk()
                if i >= 2:
                    items[i - 2][1]()
            for (qk, pv) in items[max(0, len(items) - 2):]:
                pv()

        def phase_B1(l, row_ranges):
            ar.reset()
            TB = 8
            NMAX = TB * GW
            ebint = ar.bf16(NH * 5 * 128).rearrange("p (h w) -> p h w", h=NH)
            load_table(l, rpb_int, 5, ebint, "ebint")
            pw32 = ar.f32(4 * 2 * 256)
            pwb = ar.bf16(4 * 2 * 256).rearrange("p (g k d) -> p g k d", g=4, k=2)
            S.op("sp", I("dma_start", out=pw32, in_=poolw_d[:, l * 2048:(l + 1) * 2048]), writes=["pw32"], chan="pw32")
            S.op("dve", I("tensor_copy", pwb.rearrange("p g k d -> p (g k d)"), pw32), reads=["pw32"], writes=["pwb"])
            zsl = [ar.f32(8 * (NMAX + 16)).rearrange("p (c n) -> p c n", c=8) for _ in range(2)]
            T1 = ar.f32(2 * (NMAX + 16)).rearrange("p (c n) -> p c n", c=2)
            T2 = ar.f32(2 * (NMAX + 16)).rearrange("p (c n) -> p c n", c=2)
            T3 = ar.f32(2 * 8).rearrange("p (c n) -> p c n", c=2)
            pooled = ar.bf16(8 * NMAX).rearrange("p (c n) -> p c n", c=8)
            pmst = [ar.bf16(NMAX) for _ in range(2)]
            qsl = [ar.bf16(8 * NMAX).rearrange("p (c n) -> p c n", c=8) for _ in range(2)]
            ksl = [ar.bf16(8 * (NMAX + 512)).rearrange("p (c n) -> p c n", c=8) for _ in range(2)]
            vsl = [ar.bf16((TB // 2 + 4) * 1024).rearrange("p (b f) -> p b f", f=1024) for _ in range(2)]
            Es = [ar.bf16(5 * 128) for _ in range(3)]
            Ps = [ar.bf16(5 * 128) for _ in range(3)]
            rdens = [ar.f32(128) for _ in range(2)]
            asl = [ar.bf16(8 * NMAX).rearrange("p (c n) -> p c n", c=8) for _ in range(2)]

            tiles = []
            for si, (seg, (ra, rb)) in enumerate(zip(geo.segs, row_ranges)):
                Hs = HALO
                r = ra
                while r < rb:
                    nxt = min(rb, ((r - Hs) // TB + 1) * TB + Hs)
                    tiles.append((si, seg, r, nxt - r))
                    r = nxt
            zsrc, qsrc, ksrc, vsrc = [], [], [], []
            for (si, seg, r0, nr) in tiles:
                t0 = seg["off"] + r0 * GW
                n = nr * GW
                zsrc.append(([((lambda s, n=n: s[:, :, 0:n + 16]),
                               ZT[:, :, t0 - 8:t0 + n + 8].rearrange("c p n -> p c n"))],))
                qsrc.append(([((lambda s, n=n: s[:, :, 0:n]), QT[:, :, t0:t0 + n].rearrange("c p n -> p c n"))],))
                ksrc.append(([((lambda s, n=n: s[:, :, 0:n + 512]),
                               KT[:, :, t0 - 256:t0 + n + 256].rearrange("c p n -> p c n"))],))
                vsrc.append(([((lambda s, nr=nr: s[:, 0:nr // 2 + 4, :]),
                               VV[t0 - 256:t0 + n + 256, :].rearrange("(b p) f -> p b f", p=128))],))
            zst = Stream(S, "b1z", zsl, zsrc, 1)
            qst = Stream(S, "b1q", qsl, qsrc, 1)
            kst = Stream(S, "b1k", ksl, ksrc, 1)
            vst = Stream(S, "b1v", vsl, vsrc, 1)
            it = 0
            pmn = 0
            for ti, (si, seg, r0, nr) in enumerate(tiles):
                t0 = seg["off"] + r0 * GW
                n = nr * GW
                zT, zres = zst.get(ti)
                qT, qres = qst.get(ti)
                kT, kres = kst.get(ti)
                vt, vres = vst.get(ti)
                ast = asl[ti % 2]
                astres = ("ast", ti % 2)
                items = []
                for pr in range(nr // 2):
                    for h in range(NH):
                        items.append(attn_item(l, qT, qres, pr * 128, kT, kres, pr * 128, vt, vres, pr, 5, h,
                                               ebint[:, h, :], "ebint", None, None,
                                               Es, Ps, rdens, ast, astres, pr * 128, it))
                        it += 1
                run_items(items)
                S.op("sp", I("dma_start",
                    out=AT[:, :, t0:t0 + n].rearrange("c p n -> p c n"), in_=ast[:, :, 0:n]),
                    reads=[astres], chan=f"ast{ti % 2}")
                own0 = seg["off"] + HALO * GW
                own1 = seg["off"] + (HALO + seg["R"]) * GW
                for g in range(4):
                    Z = zT[:, 2 * g:2 * g + 2, :]
                    W = n + 16
                    S.op("pool", I("tensor_tensor", T1[:, :, 1:W], Z[:, :, 0:W - 1], Z[:, :, 1:W], ALU.add),
                         reads=[zres], writes=["T1"])
                    cur, curres = T1, "T1"
                    if g >= 1:
                        S.op("pool", I("tensor_tensor", T2[:, :, 2:W - 1], T1[:, :, 1:W - 2], T1[:, :, 3:W], ALU.add),
                             reads=["T1"], writes=["T2"])
                        cur, curres = T2, "T2"
                    if g >= 2:
                        S.op("pool", I("tensor_tensor", T1[:, :, 4:W - 3], T2[:, :, 2:W - 5], T2[:, :, 6:W - 1], ALU.add),
                             reads=["T2"], writes=["T1"])
                        cur, curres = T1, "T1"
                    if g >= 3:
                        S.op("pool", I("tensor_tensor", T2[:, :, 8:W - 7], T1[:, :, 4:W - 11], T1[:, :, 12:W - 3], ALU.add),
                             reads=["T1"], writes=["T2"])
                        cur, curres = T2, "T2"
                    S.op("dve", I("scalar_tensor_tensor",
                        pooled[:, 2 * g:2 * g + 2, 0:n], cur[:, :, 8:8 + n], 1.0 / POOL_W[g], Z[:, :, 8:8 + n],
                        ALU.mult, ALU.subtract), reads=[curres, zres], writes=["pooled"])
                    for edge, tk in ((0, own0), (1, own1 - 8)):
                        if t0 <= tk and tk + 8 <= t0 + n:
                            u = tk - t0
                            ic = icorr_t[:, ((si * 2 + edge) * 4 + g) * 16:((si * 2 + edge) * 4 + g) * 16 + 16] \
                                .rearrange("p (c n) -> p c n", c=2)
                            S.op("pool", I("tensor_tensor",
                                T3[:, :, :], cur[:, :, 8 + u:16 + u], ic, ALU.mult),
                                reads=[curres, "icorr"], writes=["T3"])
                            S.op("pool", I("tensor_tensor",
                                pooled[:, 2 * g:2 * g + 2, u:u + 8], T3[:, :, :], Z[:, :, 8 + u:16 + u], ALU.subtract),
                                reads=["T3", zres], writes=["pooled"])
                for g in range(4):
                    for m in range(2):
                        b = 6 + (pmn % 2)
                        k = pmn % 2
                        pmn += 1
                        for kc in range(2):
                            S.op("pe", I("matmul",
                                bank(b, 0, n), pwb[:, g, kc, m * 128:(m + 1) * 128], pooled[:, 2 * g + kc, 0:n],
                                start=(kc == 0), stop=(kc == 1)), reads=["pwb", "pooled"], writes=[("ps", b)])
                        S.op("act", I("activation",
                            out=pmst[k][:, 0:n], in_=bank(b, 0, n), func=AF.Identity,
                            scale=pscale[:, l * 8 + 2 * g + m:l * 8 + 2 * g + m + 1]),
                            reads=[("ps", b), "pscale"], writes=[("pmst", k)])
                        S.op("sp", I("dma_start",
                            out=PM[2 * g + m, :, t0:t0 + n], in_=pmst[k][:, 0:n]),
                            reads=[("pmst", k)], chan=f"pmst{k}")
            S.barrier()

        def phase_B1s(l):
            ar.reset()
            ebf = ar.bf16(NH * 7 * 128).rearrange("p (h w) -> p h w", h=NH)
            load_table(l, rpb_exp, 7, ebf, "ebf")
            qsl = [ar.bf16(8 * 128).rearrange("p (c n) -> p c n", c=8) for _ in range(2)]
            ksl = [ar.bf16(8 * 896).rearrange("p (c n) -> p c n", c=8) for _ in range(2)]
            vsl = [ar.bf16(7 * 1024).rearrange("p (b f) -> p b f", f=1024) for _ in range(2)]
            msl = [ar.f32(7 * 128) for _ in range(2)]
            Es = [ar.bf16(7 * 128) for _ in range(3)]
            Ps = [ar.bf16(7 * 128) for _ in range(3)]
            rdens = [ar.f32(128) for _ in range(2)]
            asl = [ar.bf16(8 * 128).rearrange("p (c n) -> p c n", c=8) for _ in range(2)]
            it = 0
            n_i = 0
            for si, seg in enumerate(geo.segs):
                R = seg["R"]
                for sp, pr in enumerate((HALO, HALO + 2, HALO + R - 4, HALO + R - 2)):
                    k = n_i % 2
                    n_i += 1
                    t0 = seg["off"] + pr * GW
                    S.op("sp", I("dma_start",
                        out=qsl[k], in_=QT[:, :, t0:t0 + 128].rearrange("c p n -> p c n")),
                        writes=[("sq", k)], chan=f"sq{k}")
                    S.op("sp", I("dma_start",
                        out=ksl[k], in_=KT[:, :, t0 - 384:t0 + 512].rearrange("c p n -> p c n")),
                        writes=[("sk", k)], chan=f"sk{k}")
                    S.op("sp", I("dma_start",
                        out=vsl[k], in_=VV[t0 - 384:t0 + 512, :].rearrange("(b p) f -> p b f", p=128)),
                        writes=[("sv", k)], chan=f"sv{k}")
                    S.op("sp", I("dma_start", out=msl[k], in_=amask[si, sp]),
                         writes=[("sm", k)], chan=f"sm{k}")
                    items = []
                    for h in range(NH):
                        items.append(attn_item(l, qsl[k], ("sq", k), 0, ksl[k], ("sk", k), 0, vsl[k], ("sv", k), 0, 7, h,
                                               ebf[:, h, :], "ebf", msl[k], ("sm", k),
                                               Es, Ps, rdens, asl[k], ("sast", k), 0, it))
                        it += 1
                    run_items(items)
                    S.op("sp", I("dma_start",
                        out=AT[:, :, t0:t0 + 128].rearrange("c p n -> p c n"), in_=asl[k]),
                        reads=[("sast", k)], chan=f"sast{k}")
            S.barrier()

        def phase_B3(l, tok_ranges):
            ar.reset()
            NM = 1024
            usl = ar.bf16(KC * NM).rearrange("p (c n) -> p c n", c=KC)
            psl = ar.bf16(8 * NM).rearrange("p (c n) -> p c n", c=8)
            asl = ar.bf16(8 * NM).rearrange("p (c n) -> p c n", c=8)
            mg = ar.bf16(KC * NM).rearrange("p (c n) -> p c n", c=KC)
            w16 = [ar.bf16(KC * 128).rearrange("p (k c) -> p k c", k=KC) for _ in range(5)]
            w8 = [ar.bf16(8 * 128).rearrange("p (k c) -> p k c", k=8) for _ in range(4)]
            ta = [ar.f32(512) for _ in range(2)]
            tb_ = [ar.f32(512) for _ in range(2)]
            xsl = [ar.f32(NM) for _ in range(3)]
            tiles = []
            for seg, (a, b) in zip(geo.segs, tok_ranges):
                t = seg["off"] + a
                end = seg["off"] + b
                while t < end:
                    n = min(NM, end - t)
                    tiles.append((t, n))
                    t += n
            s16, s8, sx = [], [], []
            for (t0, n) in tiles:
                for oc in range(16):
                    s16.append(([((lambda s: s), WG[l, oc])], cdep(f"WG{l}")))
                    s16.append(([((lambda s: s), WG[l, 16 + oc])], cdep(f"WG{l}")))
                    s8.append(([((lambda s: s), WPB[l, oc])], cdep(f"WPB{l}")))
                    s8.append(([((lambda s: s), WAB[l, oc])], cdep(f"WAB{l}")))
                for oc in range(16):
                    s16.append(([((lambda s: s), WO[l, oc])], cdep(f"WO{l}")))
                    sx.append(([((lambda s, n=n: s[:, 0:n]), XT0[oc, :, t0:t0 + n])],))
            st16 = Stream(S, "cw", w16, s16, 3)
            st8 = Stream(S, "c8", w8, s8, 2)
            stx = Stream(S, "cx", xsl, sx, 1)
            i16 = i8 = ix = 0
            ev = 0
            for (t0, n) in tiles:
                chunks = subchunks(n)
                S.op("sp", I("dma_start",
                    out=usl[:, :, 0:n], in_=UT[:, :, t0:t0 + n].rearrange("c p n -> p c n")), writes=["usl"], chan="usl")
                S.op("sp", I("dma_start",
                    out=psl[:, :, 0:n], in_=PM[:, :, t0:t0 + n].rearrange("c p n -> p c n")), writes=["psl"], chan="psl")
                S.op("sp", I("dma_start",
                    out=asl[:, :, 0:n], in_=AT[:, :, t0:t0 + n].rearrange("c p n -> p c n")), writes=["asl"], chan="asl")
                for oc in range(16):
                    wgp, rgp = st16.get(i16)
                    wga, rga = st16.get(i16 + 1, oldest=i16)
                    i16 += 2
                    wpb, rpb_ = st8.get(i8)
                    wab, rab = st8.get(i8 + 1, oldest=i8)
                    i8 += 2
                    for (a, m) in chunks:
                        k = ev % 2
                        b0 = 4 * k
                        ev += 1
                        for kc in range(KC):
                            S.op("pe", I("matmul",
                                bank(b0, 0, m), wgp[:, kc, :], usl[:, kc, a:a + m], start=(kc == 0), stop=(kc == KC - 1)),
                                reads=[rgp, "usl"], writes=[("ps", b0)])
                        for kc in range(8):
                            S.op("pe", I("matmul",
                                bank(b0 + 1, 0, m), wpb[:, kc, :], psl[:, kc, a:a + m], start=(kc == 0), stop=(kc == 7)),
                                reads=[rpb_, "psl"], writes=[("ps", b0 + 1)])
                        for kc in range(KC):
                            S.op("pe", I("matmul",
                                bank(b0 + 2, 0, m), wga[:, kc, :], usl[:, kc, a:a + m], start=(kc == 0), stop=(kc == KC - 1)),
                                reads=[rga, "usl"], writes=[("ps", b0 + 2)])
                        for kc in range(8):
                            S.op("pe", I("matmul",
                                bank(b0 + 3, 0, m), wab[:, kc, :], asl[:, kc, a:a + m], start=(kc == 0), stop=(kc == 7)),
                                reads=[rab, "asl"], writes=[("ps", b0 + 3)])
                        A_, B_ = ta[k], tb_[k]
                        S.op("act", I("activation", out=A_[:, 0:m], in_=bank(b0, 0, m), func=AF.Sigmoid),
                             reads=[("ps", b0)], writes=[("ta", k)])
                        S.op("dve", I("tensor_tensor", A_[:, 0:m], A_[:, 0:m], bank(b0 + 1, 0, m), ALU.mult),
                             reads=[("ta", k), ("ps", b0 + 1)], writes=[("ta", k)])
                        S.op("act", I("activation", out=B_[:, 0:m], in_=bank(b0 + 2, 0, m), func=AF.Sigmoid),
                             reads=[("ps", b0 + 2)], writes=[("tb", k)])
                        S.op("dve", I("tensor_tensor", B_[:, 0:m], B_[:, 0:m], bank(b0 + 3, 0, m), ALU.mult),
                             reads=[("tb", k), ("ps", b0 + 3)], writes=[("tb", k)])
                        S.op("pool", I("tensor_tensor",
                            mg[:, oc, a:a + m], A_[:, 0:m], B_[:, 0:m], ALU.add),
                            reads=[("ta", k), ("tb", k)], writes=["mg"])
                for oc in range(16):
                    wo, rwo = st16.get(i16)
                    i16 += 1
                    xs, rxs = stx.get(ix)
                    kx = ix % 3
                    ix += 1
                    for (a, m) in chunks:
                        b = (ev % 2) * 4
                        ev += 1
                        for kc in range(KC):
                            S.op("pe", I("matmul",
                                bank(b, 0, m), wo[:, kc, :], mg[:, kc, a:a + m], start=(kc == 0), stop=(kc == KC - 1)),
                                reads=[rwo, "mg"], writes=[("ps", b)])
                        S.op("dve", I("tensor_tensor",
                            xs[:, a:a + m], xs[:, a:a + m], bank(b, 0, m), ALU.add),
                            reads=[rxs, ("ps", b)], writes=[rxs])
                    S.op("sp", I("dma_start", out=XM[oc, :, t0:t0 + n], in_=xs[:, 0:n]),
                         reads=[rxs], chan=f"cxs{kx}")
            S.barrier()

        def phase_C(l, tok_ranges, final):
            ar.reset()
            NM = 1022
            xm = ar.f32(KC * NM).rearrange("p (c n) -> p c n", c=KC)
            hT = ar.bf16(KC * NM).rearrange("p (c n) -> p c n", c=KC)
            gT = ar.bf16(GC * 1020).rearrange("p (c n) -> p c n", c=GC)
            sqs = [ar.bf16(1024) for _ in range(3)]
            rstd = ar.f32(1024)
            vmask = ar.f32(1024)
            wus = [ar.bf16(KC * 128).rearrange("p (k c) -> p k c", k=KC) for _ in range(4)]
            wds = [ar.bf16(GC * 128).rearrange("p (k c) -> p k c", k=GC) for _ in range(3)]
            cgs = [ar.f32(512) for _ in range(2)]
            cvs = [ar.f32(512) for _ in range(2)]
            tiles = []
            for si, (seg, (a, b)) in enumerate(zip(geo.segs, tok_ranges)):
                ch = even_chunks(b - a, 510)
                i = 0
                while i < len(ch):
                    grp = ch[i:i + 2]
                    t0 = seg["off"] + a + grp[0][0]
                    tiles.append((si, seg, t0, [(c[0] - grp[0][0], c[1]) for c in grp]))
                    i += 2
            su, sd = [], []
            for _ in tiles:
                for gi in range(NG):
                    for cp in range(GC):
                        i = gi * GC + cp
                        su.append(([((lambda s: s), WUP[l, i])], cdep(f"WUP{l}")))
                        su.append(([((lambda s: s), WUP[l, FC + i])], cdep(f"WUP{l}")))
                    for oc in range(16):
                        sd.append(([((lambda s: s), WDN[l, gi, oc])], cdep(f"WDN{l}")))
            stu = Stream(S, "fu", wus, su, 2)
            std = Stream(S, "fd", wds, sd, 2)
            iu = idn = 0
            ev = 0
            gain = g2[:, l * KC:(l + 1) * KC]
            for (si, seg, t0, chs) in tiles:
                n = sum(c[1] for c in chs)
                S.op("sp", I("dma_start",
                    out=xm[:, :, 0:n + 2], in_=XM[:, :, t0 - 1:t0 + n + 1].rearrange("c p n -> p c n")),
                    writes=["xm"], chan="xm")
                rmsnorm(xm, "xm", n + 2, t0 - 1, gain, sqs, rstd, vmask, 0,
                        lambda c, n=n: hT[:, c, 0:n + 2], "hT", "C")
                for gi in range(NG):
                    for cp in range(GC):
                        i = gi * GC + cp
                        wg, rwg = stu.get(iu)
                        wv_, rwv = stu.get(iu + 1, oldest=iu)
                        iu += 2
                        for (o, m) in chs:
                            k = ev % 2
                            ev += 1
                            bg, bv = 2 + 2 * k, 3 + 2 * k
                            for kc in range(KC):
                                S.op("pe", I("matmul",
                                    bank(bg, 0, m + 2), wg[:, kc, :], hT[:, kc, o:o + m + 2], start=(kc == 0), stop=(kc == KC - 1)),
                                    reads=[rwg, "hT"], writes=[("ps", bg)])
                            for kc in range(KC):
                                S.op("pe", I("matmul",
                                    bank(bv, 0, m + 2), wv_[:, kc, :], hT[:, kc, o:o + m + 2], start=(kc == 0), stop=(kc == KC - 1)),
                                    reads=[rwv, "hT"], writes=[("ps", bv)])
                            for (b, buf, bres, fi) in ((bg, cgs[k], ("cg", k), i), (bv, cvs[k], ("cv", k), FC + i)):
                                cw = lambda j, fi=fi: convw[:, (l * 3 + j) * 88 + fi:(l * 3 + j) * 88 + fi + 1]
                                cb = convb[:, l * 88 + fi:l * 88 + fi + 1]
                                S.op("act", I("activation",
                                    out=buf[:, 0:m], in_=bank(b, 1, m), func=AF.Identity, scale=cw(1), bias=cb),
                                    reads=[("ps", b), "convw", "convb"], writes=[bres])
                                S.op("dve", I("scalar_tensor_tensor",
                                    buf[:, 0:m], bank(b, 0, m), cw(0), buf[:, 0:m], ALU.mult, ALU.add),
                                    reads=[("ps", b), bres, "convw"], writes=[bres])
                                S.op("dve", I("scalar_tensor_tensor",
                                    buf[:, 0:m], bank(b, 2, m), cw(2), buf[:, 0:m], ALU.mult, ALU.add),
                                    reads=[("ps", b), bres, "convw"], writes=[bres])
                            S.op("act", I("activation", out=cgs[k][:, 0:m], in_=cgs[k][:, 0:m], func=AF.Gelu),
                                 reads=[("cg", k)], writes=[("cg", k)])
                            S.op("pool", I("tensor_tensor",
                                gT[:, cp, o:o + m], cgs[k][:, 0:m], cvs[k][:, 0:m], ALU.mult),
                                reads=[("cg", k), ("cv", k)], writes=["gT"])
                    for oc in range(16):
                        wd, rwd = std.get(idn)
                        idn += 1
                        for (o, m) in chs:
                            b = ev % 2
                            ev += 1
                            for cp in range(GC):
                                S.op("pe", I("matmul",
                                    bank(b, 0, m), wd[:, cp, :], gT[:, cp, o:o + m], start=(cp == 0), stop=(cp == GC - 1)),
                                    reads=[rwd, "gT"], writes=[("ps", b)])
                            S.op("dve", I("tensor_tensor",
                                xm[:, oc, 1 + o:1 + o + m], xm[:, oc, 1 + o:1 + o + m], bank(b, 0, m), ALU.add),
                                reads=["xm", ("ps", b)], writes=["xm"])
                if not final:
                    S.op("sp", I("dma_start",
                        out=XT0[:, :, t0:t0 + n].rearrange("c p n -> p c n"), in_=xm[:, :, 1:n + 1]),
                        reads=["xm"], chan="xmst")
                else:
                    xv = xm[:, :, 1:n + 1]
                    rmsnorm(xv, "xm", n, t0, gf, sqs, rstd, vmask, 0,
                            lambda c, n=n: xm[:, c, 1:n + 1], "xm", "F", masked=False)
                    ysl = [hT.rearrange("p c n -> p (c n)")[:, kk * 4096:(kk + 1) * 4096].bitcast(F32) for kk in range(2)]
                    orow = t0 - (seg["off"] + HALO * GW)
                    nb = -(-n // 128)
                    for tb in range(nb):
                        nt = min(128, n - tb * 128)
                        kk = tb % 2
                        for c4 in range(4):
                            b = 2 + (ev % 2)
                            ev += 1
                            for j in range(4):
                                c = c4 * 4 + j
                                S.op("pe", I("transpose",
                                    bank(b, j * 128, 128)[0:nt, :], xm[:, c, 1 + tb * 128:1 + tb * 128 + nt], ident),
                                    reads=["xm", "ident"], writes=[("ps", b)])
                            if c4 % 2 == 0:
                                S.op("act", I("activation",
                                    out=ysl[kk][0:nt, c4 * 512:(c4 + 1) * 512], in_=bank(b, 0, 512)[0:nt, :], func=AF.Copy),
                                    reads=[("ps", b)], writes=[("ysl", kk)])
                            else:
                                S.op("dve", I("tensor_copy",
                                    ysl[kk][0:nt, c4 * 512:(c4 + 1) * 512], bank(b, 0, 512)[0:nt, :]),
                                    reads=[("ps", b)], writes=[("ysl", kk)])
                        S.op("sp", I("dma_start",
                            out=youts[si][orow + tb * 128:orow + tb * 128 + nt, :], in_=ysl[kk][0:nt, :]),
                            reads=[("ysl", kk), "hT"], chan=f"ysl{kk}")
            S.barrier()

        H = HALO
        segs = geo.segs
        steps = [
            lambda: phase_A(0, [(0, s["E"]) for s in segs]),
            lambda: phase_B1(0, [(H - 8, H + s["R"] + 8) for s in segs]),
            lambda: phase_B1s(0),
            lambda: phase_B3(0, [((H - 6) * GW - 8, (H + s["R"] + 6) * GW + 8) for s in segs]),
            lambda: phase_C(0, [((H - 6) * GW, (H + s["R"] + 6) * GW) for s in segs], final=False),
            lambda: phase_A(1, [(H - 6, H + s["R"] + 6) for s in segs]),
            lambda: phase_B1(1, [(H - 2, H + s["R"] + 2) for s in segs]),
            lambda: phase_B1s(1),
            lambda: phase_B3(1, [(H * GW - 8, (H + s["R"]) * GW + 8) for s in segs]),
            lambda: phase_C(1, [(H * GW, (H + s["R"]) * GW) for s in segs], final=True),
        ]
        for st in steps[:stop_after]:
            st()
        S.emit()
    return nc, geo


def _col_valid():
    qc = np.arange(GW)
    cs = np.clip(qc - 8, 0, GW - 16)
    kc = np.arange(GW)
    return (kc[:, None] >= cs[None, :]) & (kc[:, None] < cs[None, :] + 16)


def _bias_tables(rpb, js, dr_lo, dr_hi):
    Ln, Hn = rpb.shape[0], rpb.shape[1]
    cv = _col_valid()
    kc = np.arange(GW)[:, None]
    qc = np.arange(GW)[None, :]
    dci = np.clip(kc - qc + 15, 0, 30)
    out = np.full((Ln, Hn, 128, len(js), 128), NEG, dtype=np.float32)
    for ji, j in enumerate(js):
        for kp in range(2):
            for qp in range(2):
                dr = 2 * j + kp - qp
                if dr < dr_lo or dr > dr_hi:
                    continue
                blk = rpb[:, :, dr + 7, :][:, :, dci]
                blk = np.where(cv[None, None], blk, np.float32(NEG))
                out[:, :, kp * 64:(kp + 1) * 64, ji, qp * 64:(qp + 1) * 64] = blk
    return out.reshape(Ln, Hn, 128, len(js) * 128)


def _core_layout(ci, RP, RS, rows_p, rows_s, x_prompt, x_sample):
    geo = Geo(RP, RS)
    npq = rows_p // RP
    nsq = rows_s // RS
    info = [(x_prompt[ci // npq], (ci % npq) * RP, rows_p), (x_sample[ci // nsq], (ci % nsq) * RS, rows_s)]
    x_ext = np.zeros((geo.NT, D), np.float32)
    valid = np.zeros((1, geo.NT), np.float32)
    icorr = np.zeros((2, 2, 4, 2, 8), np.float32)
    amask = np.zeros((2, 4, 128, 7, 128), np.float32)
    for si, (seg, (xs, row0, rows)) in enumerate(zip(geo.segs, info)):
        R, E, off = seg["R"], seg["E"], seg["off"]
        g0 = row0 - HALO
        lo = max(0, g0)
        hi = min(rows, g0 + E)
        x_ext[off + (lo - g0) * GW: off + (hi - g0) * GW] = xs[lo * GW:hi * GW]
        valid[0, off + (lo - g0) * GW: off + (hi - g0) * GW] = 1.0
        nseq = rows * GW
        for edge, tk in ((0, row0 * GW), (1, (row0 + R) * GW - 8)):
            for g, w in enumerate(POOL_W):
                t = tk + np.arange(8)
                lo_t = np.clip(t - w // 2, 0, nseq - 1)
                hi_t = np.clip(t + (w - w // 2) - 1, 0, nseq - 1)
                icorr[si, edge, g, :, :] = (1.0 / (hi_t - lo_t + 1).astype(np.float32))[None, :]
        for sp, pr in enumerate((0, 2, R - 4, R - 2)):
            for ji, j in enumerate(range(-3, 4)):
                for kp in range(2):
                    for qp in range(2):
                        r = row0 + pr + qp
                        rho = row0 + pr + 2 * j + kp
                        s = min(max(r - 4, 0), rows - 8)
                        ok = (s <= rho <= s + 7)
                        if ok:
                            amask[si, sp, kp * 64:(kp + 1) * 64, ji, qp * 64:(qp + 1) * 64] = 1.0
    icorr_b = np.ascontiguousarray(np.broadcast_to(icorr.reshape(1, -1), (128, icorr.size)))
    return dict(x_ext=x_ext, valid=valid, icorr=icorr_b, amask=amask.reshape(2, 4, 128, 7 * 128))


def _run(x_prompt, x_sample, norm1_g, w_in, pool_w, pool_scale, rpb, w_pool_br, w_attn_br, w_out,
         norm2_g, w_up, conv_w, conv_b, w_down, norm_f, debug=False):
    f = lambda a: np.ascontiguousarray(np.asarray(a, dtype=np.float32))
    x_prompt, x_sample = f(x_prompt), f(x_sample)
    rows_p = x_prompt.shape[1] // GW
    rows_s = x_sample.shape[1] // GW
    nb_p, nb_s = x_prompt.shape[0], x_sample.shape[0]
    RP = rows_p * nb_p // 8
    RS = rows_s * nb_s // 8
    nc, geo = build_program(RP, RS, debug=debug)
    rpb = f(rpb)
    common = dict(
        rpb_exp=_bias_tables(rpb, list(range(-3, 4)), -7, 7),
        rpb_int=_bias_tables(rpb, list(range(-2, 3)), -4, 3),
        ident=np.eye(128, dtype=np.float32),
        w_in=f(w_in), w_pool_br=f(w_pool_br), w_attn_br=f(w_attn_br), w_out=f(w_out), w_up=f(w_up), w_down=f(w_down),
        poolw=np.ascontiguousarray(f(pool_w).reshape(NL, 4, 2, 128, 256).transpose(3, 0, 1, 2, 4).reshape(128, -1)),
        g1=np.ascontiguousarray(f(norm1_g).reshape(NL, KC, 128).transpose(2, 0, 1).reshape(128, -1)),
        g2=np.ascontiguousarray(f(norm2_g).reshape(NL, KC, 128).transpose(2, 0, 1).reshape(128, -1)),
        gf=np.ascontiguousarray(f(norm_f).reshape(KC, 128).T),
        pscale=np.ascontiguousarray(f(pool_scale).reshape(NL, 8, 128).transpose(2, 0, 1).reshape(128, -1)),
        convw=np.ascontiguousarray(f(conv_w).reshape(NL, 3, 88, 128).transpose(3, 0, 1, 2).reshape(128, -1)),
        convb=np.ascontiguousarray(f(conv_b).reshape(NL, 88, 128).transpose(2, 0, 1).reshape(128, -1)),
    )
    in_maps = []
    for ci in range(8):
        m = dict(common)
        m.update(_core_layout(ci, RP, RS, rows_p, rows_s, x_prompt, x_sample))
        in_maps.append(m)
    res = run_bass_kernel_spmd(nc, in_maps, core_ids=list(range(8)))
    y_p = np.empty_like(x_prompt)
    y_s = np.empty_like(x_sample)
    npq = rows_p // RP
    nsq = rows_s // RS
    for ci in range(8):
        r = res.results[ci]
        y_p[ci // npq, (ci % npq) * RP * GW:((ci % npq) + 1) * RP * GW] = r["y_p"]
        y_s[ci // nsq, (ci % nsq) * RS * GW:((ci % nsq) + 1) * RS * GW] = r["y_s"]
    if debug:
        return (y_p, y_s), res, geo
    return (y_p, y_s)


def kernel(x_prompt, x_sample, norm1_g, w_in, pool_w, pool_scale, rpb, w_pool_br, w_attn_br, w_out,
           norm2_g, w_up, conv_w, conv_b, w_down, norm_f):
    return _run(x_prompt, x_sample, norm1_g, w_in, pool_w, pool_scale, rpb, w_pool_br, w_attn_br, w_out,
                norm2_g, w_up, conv_w, conv_b, w_down, norm_f)
```

```python
import numpy as np
from contextlib import ExitStack
import concourse.bass as bass
import concourse.mybir as mybir
from concourse.bass_utils import run_bass_kernel_spmd

F32 = mybir.dt.float32
BF16 = mybir.dt.bfloat16
AF = mybir.ActivationFunctionType
ALU = mybir.AluOpType
AX = mybir.AxisListType

D = 2048
KC = 16
NL = 2
GW = 64
NH = 16
FF = 5632
FC = 44
NG = 4
GC = 11
HALO = 12
EPS = 1e-6
NEG = -30000.0
POOL_W = (2, 4, 8, 16)


def I(method, *args, **kwargs):
    return lambda e: getattr(e, method)(*args, **kwargs)


class Op:
    __slots__ = ("eng", "fn", "deps", "chan", "token", "has_dep", "inc")

    def __init__(self, eng, fn, chan):
        self.eng = eng
        self.fn = fn
        self.chan = chan
        self.deps = {}
        self.token = None
        self.has_dep = False
        self.inc = False


class Sched:
    ENGS = ("pe", "act", "dve", "pool", "sp")

    def __init__(self, nc, es):
        self.nc = nc
        self.es = es
        self.ops = {e: [] for e in self.ENGS}
        self.last_write = {}
        self.readers = {}
        self.chan_cnt = {}
        self.chan_last = {}
        self.sems = {}
        self.bg_chans = set()

    def sem(self, name):
        if name not in self.sems:
            self.sems[name] = self.es.enter_context(self.nc.semaphore("s_" + name))
        return self.sems[name]

    @staticmethod
    def _key(op):
        return op.chan if op.chan is not None else op.eng

    def _add_dep(self, o, d):
        if d is None or d is o:
            return
        k = self._key(d)
        o.deps[(k, id(d))] = d

    def op(self, eng, fn, reads=(), writes=(), chan=None, extra=()):
        o = Op(eng, fn, chan)
        for r in reads:
            self._add_dep(o, self.last_write.get(r))
        for w in writes:
            self._add_dep(o, self.last_write.get(w))
            for d in self.readers.get(w, {}).values():
                self._add_dep(o, d)
        for d in extra:
            self._add_dep(o, d)
        for d in o.deps.values():
            d.has_dep = True
        for r in reads:
            self.readers.setdefault(r, {})[self._key(o)] = o
        for w in writes:
            self.last_write[w] = o
            self.readers[w] = {}
        if chan is not None:
            n = self.chan_cnt.get(chan, 0) + 1
            self.chan_cnt[chan] = n
            o.token = ("c_" + chan, 16 * n)
            self.chan_last[chan] = o
        self.ops[eng].append(o)
        return o

    def barrier(self):
        lasts = []
        for e in self.ENGS:
            for o in reversed(self.ops[e]):
                if o.fn is not None and o.chan is None:
                    lasts.append(o)
                    break
        for c, o in self.chan_last.items():
            if c not in self.bg_chans:
                lasts.append(o)
        for e in self.ENGS:
            b = Op(e, None, None)
            for d in lasts:
                if d.eng == e and d.chan is None:
                    continue
                self._add_dep(b, d)
                d.has_dep = True
            self.ops[e].append(b)
        self.last_write = {}
        self.readers = {}

    def check(self):
        sem = {}
        pos = {e: 0 for e in self.ENGS}
        progress = True
        while progress:
            progress = False
            for e in self.ENGS:
                lst = self.ops[e]
                while pos[e] < len(lst):
                    o = lst[pos[e]]
                    ok = True
                    for d in o.deps.values():
                        if d.chan is None and d.eng == "pe" and e == "pe":
                            continue
                        if d.token is None:
                            continue
                        s, v = d.token
                        if sem.get(s, 0) < v:
                            ok = False
                            break
                    if not ok:
                        break
                    if o.fn is not None:
                        if o.chan is not None:
                            sem["c_" + o.chan] = sem.get("c_" + o.chan, 0) + 16
                        elif o.inc:
                            sem["e_" + e] = sem.get("e_" + e, 0) + 1
                    pos[e] += 1
                    progress = True
        stuck = {e: (pos[e], len(self.ops[e])) for e in self.ENGS if pos[e] < len(self.ops[e])}
        if stuck:
            raise RuntimeError(f"semaphore protocol deadlock: {stuck}")
        self.n_ops = {e: len(self.ops[e]) for e in self.ENGS}

    def emit(self):
        nc = self.nc
        for e in self.ENGS:
            cnt = 0
            for o in self.ops[e]:
                if o.chan is None and o.fn is not None and o.has_dep:
                    cnt += 1
                    o.token = ("e_" + e, cnt)
                    o.inc = True
        self.check()
        for e in self.ENGS:
            self.sem("e_" + e)
        for c in self.chan_cnt:
            self.sem("c_" + c)
        sched = self

        def body_for(ename):
            def body(e):
                waited = {}
                for o in sched.ops[ename]:
                    need = {}
                    for d in o.deps.values():
                        if d.chan is None and d.eng == "pe" and ename == "pe":
                            continue
                        if d.token is None:
                            continue
                        s, v = d.token
                        if need.get(s, 0) < v:
                            need[s] = v
                    for s, v in need.items():
                        if waited.get(s, 0) < v:
                            e.wait_ge(sched.sems[s], v)
                            waited[s] = v
                    if o.fn is not None:
                        ins = o.fn(e)
                        if o.chan is not None:
                            ins.then_inc(sched.sems["c_" + o.chan], 16)
                        elif o.inc:
                            ins.then_inc(sched.sems["e_" + ename], 1)
            return body

        with nc.Block() as block:
            block.tensor(body_for("pe"))
            block.scalar(body_for("act"))
            block.vector(body_for("dve"))
            block.gpsimd(body_for("pool"))
            block.sync(body_for("sp"))


class Arena:
    def __init__(self, ap, words):
        self.ap = ap
        self.words = words
        self.base = 0
        self.pos = 0
        self.uid = 0

    def set_base(self):
        self.base = self.pos

    def reset(self):
        self.pos = self.base

    def f32(self, n):
        a = self.pos
        self.pos += n
        assert self.pos <= self.words, f"SBUF arena overflow {self.pos} > {self.words}"
        return self.ap[:, a:a + n]

    def bf16(self, n):
        w = (n + 1) // 2
        a = self.pos
        self.pos += w
        assert self.pos <= self.words, f"SBUF arena overflow {self.pos} > {self.words}"
        return self.ap[:, a:a + w].bitcast(BF16)[:, 0:n]

    def name(self, s):
        self.uid += 1
        return f"{s}#{self.uid}"


class Stream:
    def __init__(self, sched, name, slots, srcs, depth, eng="sp"):
        self.s = sched
        self.name = name
        self.slots = slots
        self.srcs = srcs
        self.depth = min(depth, len(slots) - 1)
        self.nxt = 0
        self.eng = eng

    def _issue(self, i):
        k = i % len(self.slots)
        slot = self.slots[k]
        spec = self.srcs[i]
        parts = spec[0]
        extra = spec[1] if len(spec) > 1 else ()
        for (dst_fn, src) in parts:
            dst = dst_fn(slot)
            self.s.op(self.eng, I("dma_start", out=dst, in_=src),
                      writes=[(self.name, k)], chan=f"{self.name}{k}", extra=extra)

    def get(self, i, oldest=None):
        oldest = i if oldest is None else oldest
        while self.nxt < len(self.srcs) and self.nxt <= i + self.depth:
            assert self.nxt - len(self.slots) < oldest, "stream slot still live"
            self._issue(self.nxt)
            self.nxt += 1
        k = i % len(self.slots)
        return self.slots[k], (self.name, k)


def subchunks(n, mx=512):
    out = []
    a = 0
    while a < n:
        b = min(n, a + mx)
        out.append((a, b - a))
        a = b
    return out


def even_chunks(n, mx):
    k = -(-n // mx)
    base = n // k
    rem = n - base * k
    out = []
    a = 0
    for i in range(k):
        sz = base + (1 if i < rem else 0)
        out.append((a, sz))
        a += sz
    return out


class Geo:
    def __init__(self, RP, RS):
        self.segs = []
        off = 0
        for nm, R in (("P", RP), ("S", RS)):
            E = R + 2 * HALO
            self.segs.append(dict(name=nm, R=R, E=E, off=off))
            off += E * GW
        self.NT = off


def build_program(RP, RS, debug=False, stop_after=None):
    geo = Geo(RP, RS)
    NT = geo.NT
    nc = bass.Bass("TRN2", target_bir_lowering=False)
    es = ExitStack()

    def din(name, shape, dt=F32):
        return nc.dram_tensor(name, list(shape), dt, kind="ExternalInput").ap()

    def dscr(name, shape, dt):
        if debug:
            return nc.dram_tensor(name, list(shape), dt, kind="ExternalOutput").ap()
        return nc.dram_tensor(name, list(shape), dt).ap()

    x_ext = din("x_ext", [NT, D])
    valid = din("valid", [1, NT])
    icorr = din("icorr", [128, 2 * 2 * 4 * 2 * 8])
    amask = din("amask", [2, 4, 128, 7 * 128])
    rpb_exp = din("rpb_exp", [NL, NH, 128, 7 * 128])
    rpb_int = din("rpb_int", [NL, NH, 128, 5 * 128])
    ident_d = din("ident", [128, 128])
    w_in = din("w_in", [NL, D, 8192])
    w_pb = din("w_pool_br", [NL, 1024, D])
    w_ab = din("w_attn_br", [NL, 1024, D])
    w_out = din("w_out", [NL, D, D])
    w_up = din("w_up", [NL, D, 2 * FF])
    w_dn = din("w_down", [NL, FF, D])
    poolw_d = din("poolw", [128, NL * 4 * 2 * 256])
    g1_d = din("g1", [128, NL * KC])
    g2_d = din("g2", [128, NL * KC])
    gf_d = din("gf", [128, KC])
    pscale_d = din("pscale", [128, NL * 8])
    convw_d = din("convw", [128, NL * 3 * 88])
    convb_d = din("convb", [128, NL * 88])

    y_p = nc.dram_tensor("y_p", [RP * GW, D], F32, kind="ExternalOutput").ap()
    y_s = nc.dram_tensor("y_s", [RS * GW, D], F32, kind="ExternalOutput").ap()
    youts = [y_p, y_s]

    XT0 = dscr("XT0", [KC, 128, NT], F32)
    XM = dscr("XM", [KC, 128, NT], F32)
    UT = dscr("UT", [KC, 128, NT], BF16)
    ZT = dscr("ZT", [8, 128, NT], F32)
    QT = dscr("QT", [8, 128, NT], BF16)
    KT = dscr("KT", [8, 128, NT], BF16)
    VV = dscr("VV", [NT, 1024], BF16)
    PM = dscr("PM", [8, 128, NT], BF16)
    AT = dscr("AT", [8, 128, NT], BF16)
    WA = nc.dram_tensor("WA", [NL, 24, 128, KC, 128], BF16).ap()
    WV = nc.dram_tensor("WV", [NL, 128, KC, 1024], BF16).ap()
    WG = nc.dram_tensor("WG", [NL, 32, 128, KC, 128], BF16).ap()
    WPB = nc.dram_tensor("WPB", [NL, 16, 128, 8, 128], BF16).ap()
    WAB = nc.dram_tensor("WAB", [NL, 16, 128, 8, 128], BF16).ap()
    WO = nc.dram_tensor("WO", [NL, 16, 128, KC, 128], BF16).ap()
    WUP = nc.dram_tensor("WUP", [NL, 88, 128, KC, 128], BF16).ap()
    WDN = nc.dram_tensor("WDN", [NL, NG, 16, 128, GC, 128], BF16).ap()

    with es:
        S = Sched(nc, es)
        AW = 51200
        arena_t = es.enter_context(nc.sbuf_tensor("arena", [128, AW], F32))
        ar = Arena(arena_t[:], AW)
        ps_t = es.enter_context(nc.psum_tensor("ps", [128, 8 * 512], F32))
        ps = ps_t[:]

        def bank(b, a=0, n=512):
            return ps[:, b * 512 + a: b * 512 + a + n]

        ident = ar.f32(128)
        ones_bf = ar.bf16(128)
        g1 = ar.f32(NL * KC)
        g2 = ar.f32(NL * KC)
        gf = ar.f32(KC)
        pscale = ar.f32(NL * 8)
        convw = ar.f32(NL * 3 * 88)
        convb = ar.f32(NL * 88)
        icorr_t = ar.f32(2 * 2 * 4 * 2 * 8)
        ar.set_base()

        def ld_const(dst, src, nm):
            S.op("sp", I("dma_start", out=dst, in_=src), writes=[nm], chan="const")

        ld_const(ident, ident_d[:, :], "ident")
        ld_const(g1, g1_d[:, :], "g1")
        ld_const(g2, g2_d[:, :], "g2")
        ld_const(gf, gf_d[:, :], "gf")
        ld_const(pscale, pscale_d[:, :], "pscale")
        ld_const(convw, convw_d[:, :], "convw")
        ld_const(convb, convb_d[:, :], "convb")
        ld_const(icorr_t, icorr[:, :], "icorr")
        S.op("dve", I("memset", ones_bf, 1.0), writes=["ones"])

        conv_ops = {}

        def conv(key, dst, src):
            S.bg_chans.add("cv_" + key)
            conv_ops[key] = S.op("pool", I("dma_start", out=dst, in_=src), chan="cv_" + key)

        def cols(w, l, c0, n):
            return w[l, :, c0:c0 + n].rearrange("(kc p) c -> p kc c", p=128)

        for l in range(NL):
            for oc in range(24):
                conv(f"WA{l}", WA[l, oc], cols(w_in, l, oc * 128, 128))
            conv(f"WV{l}", WV[l], cols(w_in, l, 3072, 1024))
            for oc in range(32):
                conv(f"WG{l}", WG[l, oc], cols(w_in, l, 4096 + oc * 128, 128))
            for oc in range(16):
                conv(f"WPB{l}", WPB[l, oc], cols(w_pb, l, oc * 128, 128))
                conv(f"WAB{l}", WAB[l, oc], cols(w_ab, l, oc * 128, 128))
            for oc in range(16):
                conv(f"WO{l}", WO[l, oc], cols(w_out, l, oc * 128, 128))
            for i in range(88):
                conv(f"WUP{l}", WUP[l, i], cols(w_up, l, i * 128, 128))
            for gi in range(NG):
                for oc in range(16):
                    conv(f"WDN{l}", WDN[l, gi, oc],
                         w_dn[l, gi * GC * 128:(gi + 1) * GC * 128, oc * 128:(oc + 1) * 128]
                         .rearrange("(cp p) c -> p cp c", p=128))

        def cdep(key):
            return (conv_ops[key],)

        S.barrier()

        def rmsnorm(xT, xres, n, tok0, gain, sqslots, rstd, vmask, psb, out_fn, out_res, tag, masked=True):
            rms_stats(xT, xres, n, tok0, sqslots, rstd, vmask, psb, tag, masked)
            rms_apply(xT, xres, n, gain, rstd, out_fn, out_res, tag)

        def rms_apply(xT, xres, n, gain, rstd, out_fn, out_res, tag):
            for c in range(KC):
                S.op("dve", I("scalar_tensor_tensor",
                    out_fn(c), xT[:, c, 0:n], gain[:, c:c + 1], rstd[:, 0:n], ALU.mult, ALU.mult),
                    reads=[xres, tag + "rstd"], writes=[out_res])

        def rms_stats(xT, xres, n, tok0, sqslots, rstd, vmask, psb, tag, masked=True):
            chunks = subchunks(n)
            if masked:
                S.op("sp", I("dma_start", out=vmask[:, 0:n],
                                                  in_=valid[0:1, tok0:tok0 + n].partition_broadcast(128)),
                     writes=[tag + "vm"], chan=tag + "vm")
            for c in range(KC):
                k = c % len(sqslots)
                sq = sqslots[k]
                S.op("act", I("activation", out=sq[:, 0:n], in_=xT[:, c, 0:n], func=AF.Square),
                     reads=[xres], writes=[(tag + "sq", k)])
                for si, (a, m) in enumerate(chunks):
                    S.op("pe", I("matmul",
                        bank(psb + si, 0, m), ones_bf, sq[:, a:a + m], start=(c == 0), stop=(c == KC - 1)),
                        reads=[(tag + "sq", k), "ones"], writes=[("ps", psb + si)])
            for si, (a, m) in enumerate(chunks):
                S.op("act", I("activation",
                    out=rstd[:, a:a + m], in_=bank(psb + si, 0, m), func=AF.Sqrt, scale=1.0 / D, bias=EPS),
                    reads=[("ps", psb + si)], writes=[tag + "rstd"])
            S.op("dve", I("reciprocal", rstd[:, 0:n], rstd[:, 0:n]), reads=[tag + "rstd"], writes=[tag + "rstd"])
            if masked:
                S.op("dve", I("tensor_tensor", rstd[:, 0:n], rstd[:, 0:n], vmask[:, 0:n], ALU.mult),
                     reads=[tag + "rstd", tag + "vm"], writes=[tag + "rstd"])

        def phase_A(l, row_ranges):
            ar.reset()
            wv = ar.bf16(KC * 1024).rearrange("p (k f) -> p k f", k=KC)
            xT = ar.f32(KC * 1024).rearrange("p (c n) -> p c n", c=KC)
            uT = ar.bf16(KC * 1024).rearrange("p (c n) -> p c n", c=KC)
            sqs = [ar.bf16(1024) for _ in range(3)]
            rstd = ar.f32(1024)
            vmask = ar.f32(1024)
            wslots = [ar.bf16(KC * 128).rearrange("p (k c) -> p k c", k=KC) for _ in range(4)]
            stz = [ar.f32(1024) for _ in range(2)]
            stb = [ar.bf16(1024) for _ in range(3)]
            stv = [ar.bf16(1024) for _ in range(2)]
            xin = [ar.f32(D) for _ in range(2)] if l == 0 else None
            gain = g1[:, l * KC:(l + 1) * KC]

            S.op("sp", I("dma_start", out=wv, in_=WV[l]), writes=["wv"], chan="wv", extra=cdep(f"WV{l}"))

            tiles = []
            for seg, (ra, rb) in zip(geo.segs, row_ranges):
                r = ra
                while r < rb:
                    nr = min(16, rb - r)
                    tiles.append((seg["off"] + r * GW, nr * GW))
                    r += nr
            srcs = []
            for (t0, n) in tiles:
                for oc in range(24):
                    srcs.append(([((lambda s: s), WA[l, oc])], cdep(f"WA{l}")))
            wst = Stream(S, "aw", wslots, srcs, depth=3)
            widx = 0
            ev = 0
            nst = {"z": 0, "b": 0, "v": 0}
            def stage1(ti):
                (t0, n) = tiles[ti]
                if l == 0:
                    for tb in range(n // 128):
                        k = tb % 2
                        S.op("sp", I("dma_start",
                            out=xin[k], in_=x_ext[t0 + tb * 128:t0 + (tb + 1) * 128, :]),
                            writes=[("xin", k)], chan=f"xin{k}")
                        for c4 in range(4):
                            b = 2 + (c4 % 2)
                            for j in range(4):
                                c = c4 * 4 + j
                                S.op("pe", I("transpose",
                                    bank(b, j * 128, 128), xin[k][:, c * 128:(c + 1) * 128], ident),
                                    reads=[("xin", k), "ident"], writes=[("ps", b)])
                            dst = xT[:, c4 * 4:c4 * 4 + 4, tb * 128:(tb + 1) * 128]
                            src = bank(b, 0, 512).rearrange("p (j t) -> p j t", j=4)
                            if c4 % 2 == 0:
                                S.op("act", I("activation", out=dst, in_=src, func=AF.Copy),
                                     reads=[("ps", b)], writes=["xT"])
                            else:
                                S.op("dve", I("tensor_copy", dst, src),
                                     reads=[("ps", b)], writes=["xT"])
                    S.op("sp", I("dma_start",
                        out=XT0[:, :, t0:t0 + n].rearrange("c p n -> p c n"), in_=xT[:, :, 0:n]),
                        reads=["xT"], chan="xTst")
                else:
                    S.op("sp", I("dma_start",
                        out=xT[:, :, 0:n], in_=XT0[:, :, t0:t0 + n].rearrange("c p n -> p c n")),
                        writes=["xT"], chan="xTld")
                rms_stats(xT, "xT", n, t0, sqs, rstd, vmask, 0, "A")

            stage1(0)
            for ti, (t0, n) in enumerate(tiles):
                chunks = subchunks(n)
                rms_apply(xT, "xT", n, gain, rstd, lambda c: uT[:, c, 0:n], "uT", "A")
                S.op("sp", I("dma_start",
                    out=UT[:, :, t0:t0 + n].rearrange("c p n -> p c n"), in_=uT[:, :, 0:n]),
                    reads=["uT"], chan="uTst")
                for oc in range(24):
                    w, wres = wst.get(widx)
                    widx += 1
                    if oc < 8:
                        k = nst["z"] % 2
                        nst["z"] += 1
                        stage, sres, chn = stz[k], ("stz", k), f"stz{k}"
                        dst = ZT[oc, :, t0:t0 + n]
                    else:
                        k = nst["b"] % 3
                        nst["b"] += 1
                        stage, sres, chn = stb[k], ("stb", k), f"stb{k}"
                        dst = (QT if oc < 16 else KT)[oc % 8, :, t0:t0 + n]
                    for (a, m) in chunks:
                        b = 4 + (ev % 4)
                        ev += 1
                        for kc in range(KC):
                            S.op("pe", I("matmul",
                                bank(b, 0, m), w[:, kc, :], uT[:, kc, a:a + m], start=(kc == 0), stop=(kc == KC - 1)),
                                reads=[wres, "uT"], writes=[("ps", b)])
                        sc = 0.125 if 8 <= oc < 16 else 1.0
                        if ev % 2 == 0:
                            S.op("act", I("activation",
                                out=stage[:, a:a + m], in_=bank(b, 0, m), func=AF.Copy, scale=sc),
                                reads=[("ps", b)], writes=[sres])
                        else:
                            S.op("dve", I("tensor_scalar",
                                stage[:, a:a + m], bank(b, 0, m), sc, 0.0, ALU.mult, ALU.add),
                                reads=[("ps", b)], writes=[sres])
                    S.op("sp", I("dma_start", out=dst, in_=stage[:, 0:n]),
                         reads=[sres], chan=chn)
                    if oc == 3 and ti + 1 < len(tiles):
                        stage1(ti + 1)
                for tb in range(n // 128):
                    k = nst["v"] % 2
                    nst["v"] += 1
                    stage = stv[k]
                    for vc in range(2):
                        b = 4 + (ev % 4)
                        ev += 1
                        for kc in range(KC):
                            S.op("pe", I("matmul",
                                bank(b, 0, 512), uT[:, kc, tb * 128:(tb + 1) * 128], wv[:, kc, vc * 512:(vc + 1) * 512],
                                start=(kc == 0), stop=(kc == KC - 1)),
                                reads=["wv", "uT"], writes=[("ps", b)])
                        if ev % 2 == 0:
                            S.op("act", I("activation",
                                out=stage[:, vc * 512:(vc + 1) * 512], in_=bank(b, 0, 512), func=AF.Copy),
                                reads=[("ps", b)], writes=[("stv", k)])
                        else:
                            S.op("dve", I("tensor_copy",
                                stage[:, vc * 512:(vc + 1) * 512], bank(b, 0, 512)),
                                reads=[("ps", b)], writes=[("stv", k)])
                    S.op("sp", I("dma_start",
                        out=VV[t0 + tb * 128:t0 + (tb + 1) * 128, :], in_=stage[:, 0:1024]),
                        reads=[("stv", k)], chan=f"stv{k}")
            S.barrier()

        def load_table(l, src_d, nj, dst, tag):
            stg = [ar.f32(nj * 128) for _ in range(2)]
            for h in range(NH):
                k = h % 2
                S.op("sp", I("dma_start", out=stg[k], in_=src_d[l, h]),
                     writes=[(tag + "stg", k)], chan=f"{tag}stg{k}")
                S.op("act", I("activation", out=dst[:, h, :], in_=stg[k], func=AF.Exp),
                     reads=[(tag + "stg", k)], writes=[tag])

        def attn_item(l, qT, qres, qcol, kT, kres, kcol0, vt, vres, vblk0, nj, h, tab, tabres, mask, maskres,
                      Es, Ps, rdens, ast, astres, acol, it):
            c = h // 2
            hp = (h % 2) * 64
            sset = it % 3
            sb = 2 * sset
            ob = 6 + (it % 2)
            E = Es[it % 3]
            P = Ps[it % 3]
            rden = rdens[it % 2]
            W = nj * 128
            meng = "pool" if (it % 3 == 2) else "dve"

            def qk():
                for idx in range(nj):
                    S.op("pe", I("matmul",
                        ps[:, sb * 512 + idx * 128: sb * 512 + (idx + 1) * 128],
                        kT[hp:hp + 64, c, kcol0 + idx * 128: kcol0 + (idx + 1) * 128],
                        qT[hp:hp + 64, c, qcol:qcol + 128], start=True, stop=True),
                        reads=[kres, qres], writes=[("pss", sset)])
                S.op("act", I("activation", out=E[:, 0:W], in_=ps[:, sb * 512: sb * 512 + W], func=AF.Exp),
                     reads=[("pss", sset)], writes=[("E", it % 3)])
                S.op(meng, I("tensor_tensor", P[:, 0:W], E[:, 0:W], tab, ALU.mult),
                     reads=[("E", it % 3), tabres], writes=[("P", it % 3)])
                if mask is not None:
                    S.op(meng, I("tensor_tensor", P[:, 0:W], P[:, 0:W], mask, ALU.mult),
                         reads=[("P", it % 3), maskres], writes=[("P", it % 3)])

            def pv():
                ores = ("ps", ob)
                for idx in range(nj):
                    S.op("pe", I("matmul",
                        bank(ob, 0, 128), vt[:, vblk0 + idx, c * 128:(c + 1) * 128],
                        P[:, idx * 128:(idx + 1) * 128], start=(idx == 0), stop=(idx == nj - 1)),
                        reads=[vres, ("P", it % 3)], writes=[ores])
                for idx in range(nj):
                    S.op("pe", I("matmul",
                        bank(ob, 128, 128), ones_bf, P[:, idx * 128:(idx + 1) * 128],
                        start=(idx == 0), stop=(idx == nj - 1)),
                        reads=["ones", ("P", it % 3)], writes=[ores])
                S.op("dve", I("reciprocal", rden[hp:hp + 64, :], bank(ob, 128, 128)[hp:hp + 64, :]),
                     reads=[ores], writes=[("rden", it % 2)])
                S.op("dve", I("tensor_tensor",
                    ast[hp:hp + 64, c, acol:acol + 128], bank(ob, 0, 128)[hp:hp + 64, :], rden[hp:hp + 64, :],
                    ALU.mult), reads=[ores, ("rden", it % 2)], writes=[astres])
            return qk, pv

        def run_items(items):
            for i, (qk, pv) in enumerate(items):
                qk()
                if i >= 2:
                    items[i - 2][1]()
            for (qk, pv) in items[max(0, len(items) - 2):]:
                pv()

        def phase_B1(l, row_ranges):
            ar.reset()
            TB = 8
            NMAX = TB * GW
            ebint = ar.bf16(NH * 5 * 128).rearrange("p (h w) -> p h w", h=NH)
            load_table(l, rpb_int, 5, ebint, "ebint")
            pw32 = ar.f32(4 * 2 * 256)
            pwb = ar.bf16(4 * 2 * 256).rearrange("p (g k d) -> p g k d", g=4, k=2)
            S.op("sp", I("dma_start", out=pw32, in_=poolw_d[:, l * 2048:(l + 1) * 2048]), writes=["pw32"], chan="pw32")
            S.op("dve", I("tensor_copy", pwb.rearrange("p g k d -> p (g k d)"), pw32), reads=["pw32"], writes=["pwb"])
            zsl = [ar.f32(8 * (NMAX + 16)).rearrange("p (c n) -> p c n", c=8) for _ in range(2)]
            T1 = ar.f32(2 * (NMAX + 16)).rearrange("p (c n) -> p c n", c=2)
            T2 = ar.f32(2 * (NMAX + 16)).rearrange("p (c n) -> p c n", c=2)
            T3 = ar.f32(2 * 8).rearrange("p (c n) -> p c n", c=2)
            pooled = ar.bf16(8 * NMAX).rearrange("p (c n) -> p c n", c=8)
            pmst = [ar.bf16(NMAX) for _ in range(2)]
            qsl = [ar.bf16(8 * NMAX).rearrange("p (c n) -> p c n", c=8) for _ in range(2)]
            ksl = [ar.bf16(8 * (NMAX + 512)).rearrange("p (c n) -> p c n", c=8) for _ in range(2)]
            vsl = [ar.bf16((TB // 2 + 4) * 1024).rearrange("p (b f) -> p b f", f=1024) for _ in range(2)]
            Es = [ar.bf16(5 * 128) for _ in range(3)]
            Ps = [ar.bf16(5 * 128) for _ in range(3)]
            rdens = [ar.f32(128) for _ in range(2)]
            asl = [ar.bf16(8 * NMAX).rearrange("p (c n) -> p c n", c=8) for _ in range(2)]

            tiles = []
            for si, (seg, (ra, rb)) in enumerate(zip(geo.segs, row_ranges)):
                Hs = HALO
                r = ra
                while r < rb:
                    nxt = min(rb, ((r - Hs) // TB + 1) * TB + Hs)
                    tiles.append((si, seg, r, nxt - r))
                    r = nxt
            zsrc, qsrc, ksrc, vsrc = [], [], [], []
            for (si, seg, r0, nr) in tiles:
                t0 = seg["off"] + r0 * GW
                n = nr * GW
                zsrc.append(([((lambda s, n=n: s[:, :, 0:n + 16]),
                               ZT[:, :, t0 - 8:t0 + n + 8].rearrange("c p n -> p c n"))],))
                qsrc.append(([((lambda s, n=n: s[:, :, 0:n]), QT[:, :, t0:t0 + n].rearrange("c p n -> p c n"))],))
                ksrc.append(([((lambda s, n=n: s[:, :, 0:n + 512]),
                               KT[:, :, t0 - 256:t0 + n + 256].rearrange("c p n -> p c n"))],))
                vsrc.append(([((lambda s, nr=nr: s[:, 0:nr // 2 + 4, :]),
                               VV[t0 - 256:t0 + n + 256, :].rearrange("(b p) f -> p b f", p=128))],))
            zst = Stream(S, "b1z", zsl, zsrc, 1)
            qst = Stream(S, "b1q", qsl, qsrc, 1)
            kst = Stream(S, "b1k", ksl, ksrc, 1)
            vst = Stream(S, "b1v", vsl, vsrc, 1)
            it = 0
            pmn = 0
            for ti, (si, seg, r0, nr) in enumerate(tiles):
                t0 = seg["off"] + r0 * GW
                n = nr * GW
                zT, zres = zst.get(ti)
                qT, qres = qst.get(ti)
                kT, kres = kst.get(ti)
                vt, vres = vst.get(ti)
                ast = asl[ti % 2]
                astres = ("ast", ti % 2)
                items = []
                for pr in range(nr // 2):
                    for h in range(NH):
                        items.append(attn_item(l, qT, qres, pr * 128, kT, kres, pr * 128, vt, vres, pr, 5, h,
                                               ebint[:, h, :], "ebint", None, None,
                                               Es, Ps, rdens, ast, astres, pr * 128, it))
                        it += 1
                run_items(items)
                S.op("sp", I("dma_start",
                    out=AT[:, :, t0:t0 + n].rearrange("c p n -> p c n"), in_=ast[:, :, 0:n]),
                    reads=[astres], chan=f"ast{ti % 2}")
                own0 = seg["off"] + HALO * GW
                own1 = seg["off"] + (HALO + seg["R"]) * GW
                for g in range(4):
                    Z = zT[:, 2 * g:2 * g + 2, :]
                    W = n + 16
                    S.op("pool", I("tensor_tensor", T1[:, :, 1:W], Z[:, :, 0:W - 1], Z[:, :, 1:W], ALU.add),
                         reads=[zres], writes=["T1"])
                    cur, curres = T1, "T1"
                    if g >= 1:
                        S.op("pool", I("tensor_tensor", T2[:, :, 2:W - 1], T1[:, :, 1:W - 2], T1[:, :, 3:W], ALU.add),
                             reads=["T1"], writes=["T2"])
                        cur, curres = T2, "T2"
                    if g >= 2:
                        S.op("pool", I("tensor_tensor", T1[:, :, 4:W - 3], T2[:, :, 2:W - 5], T2[:, :, 6:W - 1], ALU.add),
                             reads=["T2"], writes=["T1"])
                        cur, curres = T1, "T1"
                    if g >= 3:
                        S.op("pool", I("tensor_tensor", T2[:, :, 8:W - 7], T1[:, :, 4:W - 11], T1[:, :, 12:W - 3], ALU.add),
                             reads=["T1"], writes=["T2"])
                        cur, curres = T2, "T2"
                    S.op("dve", I("scalar_tensor_tensor",
                        pooled[:, 2 * g:2 * g + 2, 0:n], cur[:, :, 8:8 + n], 1.0 / POOL_W[g], Z[:, :, 8:8 + n],
                        ALU.mult, ALU.subtract), reads=[curres, zres], writes=["pooled"])
                    for edge, tk in ((0, own0), (1, own1 - 8)):
                        if t0 <= tk and tk + 8 <= t0 + n:
                            u = tk - t0
                            ic = icorr_t[:, ((si * 2 + edge) * 4 + g) * 16:((si * 2 + edge) * 4 + g) * 16 + 16] \
                                .rearrange("p (c n) -> p c n", c=2)
                            S.op("pool", I("tensor_tensor",
                                T3[:, :, :], cur[:, :, 8 + u:16 + u], ic, ALU.mult),
                                reads=[curres, "icorr"], writes=["T3"])
                            S.op("pool", I("tensor_tensor",
                                pooled[:, 2 * g:2 * g + 2, u:u + 8], T3[:, :, :], Z[:, :, 8 + u:16 + u], ALU.subtract),
                                reads=["T3", zres], writes=["pooled"])
                for g in range(4):
                    for m in range(2):
                        b = 6 + (pmn % 2)
                        k = pmn % 2
                        pmn += 1
                        for kc in range(2):
                            S.op("pe", I("matmul",
                                bank(b, 0, n), pwb[:, g, kc, m * 128:(m + 1) * 128], pooled[:, 2 * g + kc, 0:n],
                                start=(kc == 0), stop=(kc == 1)), reads=["pwb", "pooled"], writes=[("ps", b)])
                        S.op("act", I("activation",
                            out=pmst[k][:, 0:n], in_=bank(b, 0, n), func=AF.Identity,
                            scale=pscale[:, l * 8 + 2 * g + m:l * 8 + 2 * g + m + 1]),
                            reads=[("ps", b), "pscale"], writes=[("pmst", k)])
                        S.op("sp", I("dma_start",
                            out=PM[2 * g + m, :, t0:t0 + n], in_=pmst[k][:, 0:n]),
                            reads=[("pmst", k)], chan=f"pmst{k}")
            S.barrier()

        def phase_B1s(l):
            ar.reset()
            ebf = ar.bf16(NH * 7 * 128).rearrange("p (h w) -> p h w", h=NH)
            load_table(l, rpb_exp, 7, ebf, "ebf")
            qsl = [ar.bf16(8 * 128).rearrange("p (c n) -> p c n", c=8) for _ in range(2)]
            ksl = [ar.bf16(8 * 896).rearrange("p (c n) -> p c n", c=8) for _ in range(2)]
            vsl = [ar.bf16(7 * 1024).rearrange("p (b f) -> p b f", f=1024) for _ in range(2)]
            msl = [ar.f32(7 * 128) for _ in range(2)]
            Es = [ar.bf16(7 * 128) for _ in range(3)]
            Ps = [ar.bf16(7 * 128) for _ in range(3)]
            rdens = [ar.f32(128) for _ in range(2)]
            asl = [ar.bf16(8 * 128).rearrange("p (c n) -> p c n", c=8) for _ in range(2)]
            it = 0
            n_i = 0
            for si, seg in enumerate(geo.segs):
                R = seg["R"]
                for sp, pr in enumerate((HALO, HALO + 2, HALO + R - 4, HALO + R - 2)):
                    k = n_i % 2
                    n_i += 1
                    t0 = seg["off"] + pr * GW
                    S.op("sp", I("dma_start",
                        out=qsl[k], in_=QT[:, :, t0:t0 + 128].rearrange("c p n -> p c n")),
                        writes=[("sq", k)], chan=f"sq{k}")
                    S.op("sp", I("dma_start",
                        out=ksl[k], in_=KT[:, :, t0 - 384:t0 + 512].rearrange("c p n -> p c n")),
                        writes=[("sk", k)], chan=f"sk{k}")
                    S.op("sp", I("dma_start",
                        out=vsl[k], in_=VV[t0 - 384:t0 + 512, :].rearrange("(b p) f -> p b f", p=128)),
                        writes=[("sv", k)], chan=f"sv{k}")
                    S.op("sp", I("dma_start", out=msl[k], in_=amask[si, sp]),
                         writes=[("sm", k)], chan=f"sm{k}")
                    items = []
                    for h in range(NH):
                        items.append(attn_item(l, qsl[k], ("sq", k), 0, ksl[k], ("sk", k), 0, vsl[k], ("sv", k), 0, 7, h,
                                               ebf[:, h, :], "ebf", msl[k], ("sm", k),
                                               Es, Ps, rdens, asl[k], ("sast", k), 0, it))
                        it += 1
                    run_items(items)
                    S.op("sp", I("dma_start",
                        out=AT[:, :, t0:t0 + 128].rearrange("c p n -> p c n"), in_=asl[k]),
                        reads=[("sast", k)], chan=f"sast{k}")
            S.barrier()

        def phase_B3(l, tok_ranges):
            ar.reset()
            NM = 1024
            usl = ar.bf16(KC * NM).rearrange("p (c n) -> p c n", c=KC)
            psl = ar.bf16(8 * NM).rearrange("p (c n) -> p c n", c=8)
            asl = ar.bf16(8 * NM).rearrange("p (c n) -> p c n", c=8)
            mg = ar.bf16(KC * NM).rearrange("p (c n) -> p c n", c=KC)
            w16 = [ar.bf16(KC * 128).rearrange("p (k c) -> p k c", k=KC) for _ in range(5)]
            w8 = [ar.bf16(8 * 128).rearrange("p (k c) -> p k c", k=8) for _ in range(4)]
            ta = [ar.f32(512) for _ in range(2)]
            tb_ = [ar.f32(512) for _ in range(2)]
            xsl = [ar.f32(NM) for _ in range(3)]
            tiles = []
            for seg, (a, b) in zip(geo.segs, tok_ranges):
                t = seg["off"] + a
                end = seg["off"] + b
                while t < end:
                    n = min(NM, end - t)
                    tiles.append((t, n))
                    t += n
            s16, s8, sx = [], [], []
            for (t0, n) in tiles:
                for oc in range(16):
                    s16.append(([((lambda s: s), WG[l, oc])], cdep(f"WG{l}")))
                    s16.append(([((lambda s: s), WG[l, 16 + oc])], cdep(f"WG{l}")))
                    s8.append(([((lambda s: s), WPB[l, oc])], cdep(f"WPB{l}")))
                    s8.append(([((lambda s: s), WAB[l, oc])], cdep(f"WAB{l}")))
                for oc in range(16):
                    s16.append(([((lambda s: s), WO[l, oc])], cdep(f"WO{l}")))
                    sx.append(([((lambda s, n=n: s[:, 0:n]), XT0[oc, :, t0:t0 + n])],))
            st16 = Stream(S, "cw", w16, s16, 3)
            st8 = Stream(S, "c8", w8, s8, 2)
            stx = Stream(S, "cx", xsl, sx, 1)
            i16 = i8 = ix = 0
            ev = 0
            for (t0, n) in tiles:
                chunks = subchunks(n)
                S.op("sp", I("dma_start",
                    out=usl[:, :, 0:n], in_=UT[:, :, t0:t0 + n].rearrange("c p n -> p c n")), writes=["usl"], chan="usl")
                S.op("sp", I("dma_start",
                    out=psl[:, :, 0:n], in_=PM[:, :, t0:t0 + n].rearrange("c p n -> p c n")), writes=["psl"], chan="psl")
                S.op("sp", I("dma_start",
                    out=asl[:, :, 0:n], in_=AT[:, :, t0:t0 + n].rearrange("c p n -> p c n")), writes=["asl"], chan="asl")
                for oc in range(16):
                    wgp, rgp = st16.get(i16)
                    wga, rga = st16.get(i16 + 1, oldest=i16)
                    i16 += 2
                    wpb, rpb_ = st8.get(i8)
                    wab, rab = st8.get(i8 + 1, oldest=i8)
                    i8 += 2
                    for (a, m) in chunks:
                        k = ev % 2
                        b0 = 4 * k
                        ev += 1
                        for kc in range(KC):
                            S.op("pe", I("matmul",
                                bank(b0, 0, m), wgp[:, kc, :], usl[:, kc, a:a + m], start=(kc == 0), stop=(kc == KC - 1)),
                                reads=[rgp, "usl"], writes=[("ps", b0)])
                        for kc in range(8):
                            S.op("pe", I("matmul",
                                bank(b0 + 1, 0, m), wpb[:, kc, :], psl[:, kc, a:a + m], start=(kc == 0), stop=(kc == 7)),
                                reads=[rpb_, "psl"], writes=[("ps", b0 + 1)])
                        for kc in range(KC):
                            S.op("pe", I("matmul",
                                bank(b0 + 2, 0, m), wga[:, kc, :], usl[:, kc, a:a + m], start=(kc == 0), stop=(kc == KC - 1)),
                                reads=[rga, "usl"], writes=[("ps", b0 + 2)])
                        for kc in range(8):
                            S.op("pe", I("matmul",
                                bank(b0 + 3, 0, m), wab[:, kc, :], asl[:, kc, a:a + m], start=(kc == 0), stop=(kc == 7)),
                                reads=[rab, "asl"], writes=[("ps", b0 + 3)])
                        A_, B_ = ta[k], tb_[k]
                        S.op("act", I("activation", out=A_[:, 0:m], in_=bank(b0, 0, m), func=AF.Sigmoid),
                             reads=[("ps", b0)], writes=[("ta", k)])
                        S.op("dve", I("tensor_tensor", A_[:, 0:m], A_[:, 0:m], bank(b0 + 1, 0, m), ALU.mult),
                             reads=[("ta", k), ("ps", b0 + 1)], writes=[("ta", k)])
                        S.op("act", I("activation", out=B_[:, 0:m], in_=bank(b0 + 2, 0, m), func=AF.Sigmoid),
                             reads=[("ps", b0 + 2)], writes=[("tb", k)])
                        S.op("dve", I("tensor_tensor", B_[:, 0:m], B_[:, 0:m], bank(b0 + 3, 0, m), ALU.mult),
                             reads=[("tb", k), ("ps", b0 + 3)], writes=[("tb", k)])
                        S.op("pool", I("tensor_tensor",
                            mg[:, oc, a:a + m], A_[:, 0:m], B_[:, 0:m], ALU.add),
                            reads=[("ta", k), ("tb", k)], writes=["mg"])
                for oc in range(16):
                    wo, rwo = st16.get(i16)
                    i16 += 1
                    xs, rxs = stx.get(ix)
                    kx = ix % 3
                    ix += 1
                    for (a, m) in chunks:
                        b = (ev % 2) * 4
                        ev += 1
                        for kc in range(KC):
                            S.op("pe", I("matmul",
                                bank(b, 0, m), wo[:, kc, :], mg[:, kc, a:a + m], start=(kc == 0), stop=(kc == KC - 1)),
                                reads=[rwo, "mg"], writes=[("ps", b)])
                        S.op("dve", I("tensor_tensor",
                            xs[:, a:a + m], xs[:, a:a + m], bank(b, 0, m), ALU.add),
                            reads=[rxs, ("ps", b)], writes=[rxs])
                    S.op("sp", I("dma_start", out=XM[oc, :, t0:t0 + n], in_=xs[:, 0:n]),
                         reads=[rxs], chan=f"cxs{kx}")
            S.barrier()

        def phase_C(l, tok_ranges, final):
            ar.reset()
            NM = 1022
            xm = ar.f32(KC * NM).rearrange("p (c n) -> p c n", c=KC)
            hT = ar.bf16(KC * NM).rearrange("p (c n) -> p c n", c=KC)
            gT = ar.bf16(GC * 1020).rearrange("p (c n) -> p c n", c=GC)
            sqs = [ar.bf16(1024) for _ in range(3)]
            rstd = ar.f32(1024)
            vmask = ar.f32(1024)
            wus = [ar.bf16(KC * 128).rearrange("p (k c) -> p k c", k=KC) for _ in range(4)]
            wds = [ar.bf16(GC * 128).rearrange("p (k c) -> p k c", k=GC) for _ in range(3)]
            cgs = [ar.f32(512) for _ in range(2)]
            cvs = [ar.f32(512) for _ in range(2)]
            tiles = []
            for si, (seg, (a, b)) in enumerate(zip(geo.segs, tok_ranges)):
                ch = even_chunks(b - a, 510)
                i = 0
                while i < len(ch):
                    grp = ch[i:i + 2]
                    t0 = seg["off"] + a + grp[0][0]
                    tiles.append((si, seg, t0, [(c[0] - grp[0][0], c[1]) for c in grp]))
                    i += 2
            su, sd = [], []
            for _ in tiles:
                for gi in range(NG):
                    for cp in range(GC):
                        i = gi * GC + cp
                        su.append(([((lambda s: s), WUP[l, i])], cdep(f"WUP{l}")))
                        su.append(([((lambda s: s), WUP[l, FC + i])], cdep(f"WUP{l}")))
                    for oc in range(16):
                        sd.append(([((lambda s: s), WDN[l, gi, oc])], cdep(f"WDN{l}")))
            stu = Stream(S, "fu", wus, su, 2)
            std = Stream(S, "fd", wds, sd, 2)
            iu = idn = 0
            ev = 0
            gain = g2[:, l * KC:(l + 1) * KC]
            for (si, seg, t0, chs) in tiles:
                n = sum(c[1] for c in chs)
                S.op("sp", I("dma_start",
                    out=xm[:, :, 0:n + 2], in_=XM[:, :, t0 - 1:t0 + n + 1].rearrange("c p n -> p c n")),
                    writes=["xm"], chan="xm")
                rmsnorm(xm, "xm", n + 2, t0 - 1, gain, sqs, rstd, vmask, 0,
                        lambda c, n=n: hT[:, c, 0:n + 2], "hT", "C")
                for gi in range(NG):
                    for cp in range(GC):
                        i = gi * GC + cp
                        wg, rwg = stu.get(iu)
                        wv_, rwv = stu.get(iu + 1, oldest=iu)
                        iu += 2
                        for (o, m) in chs:
                            k = ev % 2
                            ev += 1
                            bg, bv = 2 + 2 * k, 3 + 2 * k
                            for kc in range(KC):
                                S.op("pe", I("matmul",
                                    bank(bg, 0, m + 2), wg[:, kc, :], hT[:, kc, o:o + m + 2], start=(kc == 0), stop=(kc == KC - 1)),
                                    reads=[rwg, "hT"], writes=[("ps", bg)])
                            for kc in range(KC):
                                S.op("pe", I("matmul",
                                    bank(bv, 0, m + 2), wv_[:, kc, :], hT[:, kc, o:o + m + 2], start=(kc == 0), stop=(kc == KC - 1)),
                                    reads=[rwv, "hT"], writes=[("ps", bv)])
                            for (b, buf, bres, fi) in ((bg, cgs[k], ("cg", k), i), (bv, cvs[k], ("cv", k), FC + i)):
                                cw = lambda j, fi=fi: convw[:, (l * 3 + j) * 88 + fi:(l * 3 + j) * 88 + fi + 1]
                                cb = convb[:, l * 88 + fi:l * 88 + fi + 1]
                                S.op("act", I("activation",
                                    out=buf[:, 0:m], in_=bank(b, 1, m), func=AF.Identity, scale=cw(1), bias=cb),
                                    reads=[("ps", b), "convw", "convb"], writes=[bres])
                                S.op("dve", I("scalar_tensor_tensor",
                                    buf[:, 0:m], bank(b, 0, m), cw(0), buf[:, 0:m], ALU.mult, ALU.add),
                                    reads=[("ps", b), bres, "convw"], writes=[bres])
                                S.op("dve", I("scalar_tensor_tensor",
                                    buf[:, 0:m], bank(b, 2, m), cw(2), buf[:, 0:m], ALU.mult, ALU.add),
                                    reads=[("ps", b), bres, "convw"], writes=[bres])
                            S.op("act", I("activation", out=cgs[k][:, 0:m], in_=cgs[k][:, 0:m], func=AF.Gelu),
                                 reads=[("cg", k)], writes=[("cg", k)])
                            S.op("pool", I("tensor_tensor",
                                gT[:, cp, o:o + m], cgs[k][:, 0:m], cvs[k][:, 0:m], ALU.mult),
                                reads=[("cg", k), ("cv", k)], writes=["gT"])
                    for oc in range(16):
                        wd, rwd = std.get(idn)
                        idn += 1
                        for (o, m) in chs:
                            b = ev % 2
                            ev += 1
                            for cp in range(GC):
                                S.op("pe", I("matmul",
                                    bank(b, 0, m), wd[:, cp, :], gT[:, cp, o:o + m], start=(cp == 0), stop=(cp == GC - 1)),
                                    reads=[rwd, "gT"], writes=[("ps", b)])
                            S.op("dve", I("tensor_tensor",
                                xm[:, oc, 1 + o:1 + o + m], xm[:, oc, 1 + o:1 + o + m], bank(b, 0, m), ALU.add),
                                reads=["xm", ("ps", b)], writes=["xm"])
                if not final:
                    S.op("sp", I("dma_start",
                        out=XT0[:, :, t0:t0 + n].rearrange("c p n -> p c n"), in_=xm[:, :, 1:n + 1]),
                        reads=["xm"], chan="xmst")
                else:
                    xv = xm[:, :, 1:n + 1]
                    rmsnorm(xv, "xm", n, t0, gf, sqs, rstd, vmask, 0,
                            lambda c, n=n: xm[:, c, 1:n + 1], "xm", "F", masked=False)
                    ysl = [hT.rearrange("p c n -> p (c n)")[:, kk * 4096:(kk + 1) * 4096].bitcast(F32) for kk in range(2)]
                    orow = t0 - (seg["off"] + HALO * GW)
                    nb = -(-n // 128)
                    for tb in range(nb):
                        nt = min(128, n - tb * 128)
                        kk = tb % 2
                        for c4 in range(4):
                            b = 2 + (ev % 2)
                            ev += 1
                            for j in range(4):
                                c = c4 * 4 + j
                                S.op("pe", I("transpose",
                                    bank(b, j * 128, 128)[0:nt, :], xm[:, c, 1 + tb * 128:1 + tb * 128 + nt], ident),
                                    reads=["xm", "ident"], writes=[("ps", b)])
                            if c4 % 2 == 0:
                                S.op("act", I("activation",
                                    out=ysl[kk][0:nt, c4 * 512:(c4 + 1) * 512], in_=bank(b, 0, 512)[0:nt, :], func=AF.Copy),
                                    reads=[("ps", b)], writes=[("ysl", kk)])
                            else:
                                S.op("dve", I("tensor_copy",
                                    ysl[kk][0:nt, c4 * 512:(c4 + 1) * 512], bank(b, 0, 512)[0:nt, :]),
                                    reads=[("ps", b)], writes=[("ysl", kk)])
                        S.op("sp", I("dma_start",
                            out=youts[si][orow + tb * 128:orow + tb * 128 + nt, :], in_=ysl[kk][0:nt, :]),
                            reads=[("ysl", kk), "hT"], chan=f"ysl{kk}")
            S.barrier()

        H = HALO
        segs = geo.segs
        steps = [
            lambda: phase_A(0, [(0, s["E"]) for s in segs]),
            lambda: phase_B1(0, [(H - 8, H + s["R"] + 8) for s in segs]),
            lambda: phase_B1s(0),
            lambda: phase_B3(0, [((H - 6) * GW - 8, (H + s["R"] + 6) * GW + 8) for s in segs]),
            lambda: phase_C(0, [((H - 6) * GW, (H + s["R"] + 6) * GW) for s in segs], final=False),
            lambda: phase_A(1, [(H - 6, H + s["R"] + 6) for s in segs]),
            lambda: phase_B1(1, [(H - 2, H + s["R"] + 2) for s in segs]),
            lambda: phase_B1s(1),
            lambda: phase_B3(1, [(H * GW - 8, (H + s["R"]) * GW + 8) for s in segs]),
            lambda: phase_C(1, [(H * GW, (H + s["R"]) * GW) for s in segs], final=True),
        ]
        for st in steps[:stop_after]:
            st()
        S.emit()
    return nc, geo


def _col_valid():
    qc = np.arange(GW)
    cs = np.clip(qc - 8, 0, GW - 16)
    kc = np.arange(GW)
    return (kc[:, None] >= cs[None, :]) & (kc[:, None] < cs[None, :] + 16)


def _bias_tables(rpb, js, dr_lo, dr_hi):
    Ln, Hn = rpb.shape[0], rpb.shape[1]
    cv = _col_valid()
    kc = np.arange(GW)[:, None]
    qc = np.arange(GW)[None, :]
    dci = np.clip(kc - qc + 15, 0, 30)
    out = np.full((Ln, Hn, 128, len(js), 128), NEG, dtype=np.float32)
    for ji, j in enumerate(js):
        for kp in range(2):
            for qp in range(2):
                dr = 2 * j + kp - qp
                if dr < dr_lo or dr > dr_hi:
                    continue
                blk = rpb[:, :, dr + 7, :][:, :, dci]
                blk = np.where(cv[None, None], blk, np.float32(NEG))
                out[:, :, kp * 64:(kp + 1) * 64, ji, qp * 64:(qp + 1) * 64] = blk
    return out.reshape(Ln, Hn, 128, len(js) * 128)


def _core_layout(ci, RP, RS, rows_p, rows_s, x_prompt, x_sample):
    geo = Geo(RP, RS)
    npq = rows_p // RP
    nsq = rows_s // RS
    info = [(x_prompt[ci // npq], (ci % npq) * RP, rows_p), (x_sample[ci // nsq], (ci % nsq) * RS, rows_s)]
    x_ext = np.zeros((geo.NT, D), np.float32)
    valid = np.zeros((1, geo.NT), np.float32)
    icorr = np.zeros((2, 2, 4, 2, 8), np.float32)
    amask = np.zeros((2, 4, 128, 7, 128), np.float32)
    for si, (seg, (xs, row0, rows)) in enumerate(zip(geo.segs, info)):
        R, E, off = seg["R"], seg["E"], seg["off"]
        g0 = row0 - HALO
        lo = max(0, g0)
        hi = min(rows, g0 + E)
        x_ext[off + (lo - g0) * GW: off + (hi - g0) * GW] = xs[lo * GW:hi * GW]
        valid[0, off + (lo - g0) * GW: off + (hi - g0) * GW] = 1.0
        nseq = rows * GW
        for edge, tk in ((0, row0 * GW), (1, (row0 + R) * GW - 8)):
            for g, w in enumerate(POOL_W):
                t = tk + np.arange(8)
                lo_t = np.clip(t - w // 2, 0, nseq - 1)
                hi_t = np.clip(t + (w - w // 2) - 1, 0, nseq - 1)
                icorr[si, edge, g, :, :] = (1.0 / (hi_t - lo_t + 1).astype(np.float32))[None, :]
        for sp, pr in enumerate((0, 2, R - 4, R - 2)):
            for ji, j in enumerate(range(-3, 4)):
                for kp in range(2):
                    for qp in range(2):
                        r = row0 + pr + qp
                        rho = row0 + pr + 2 * j + kp
                        s = min(max(r - 4, 0), rows - 8)
                        ok = (s <= rho <= s + 7)
                        if ok:
                            amask[si, sp, kp * 64:(kp + 1) * 64, ji, qp * 64:(qp + 1) * 64] = 1.0
    icorr_b = np.ascontiguousarray(np.broadcast_to(icorr.reshape(1, -1), (128, icorr.size)))
    return dict(x_ext=x_ext, valid=valid, icorr=icorr_b, amask=amask.reshape(2, 4, 128, 7 * 128))


def _run(x_prompt, x_sample, norm1_g, w_in, pool_w, pool_scale, rpb, w_pool_br, w_attn_br, w_out,
         norm2_g, w_up, conv_w, conv_b, w_down, norm_f, debug=False):
    f = lambda a: np.ascontiguousarray(np.asarray(a, dtype=np.float32))
    x_prompt, x_sample = f(x_prompt), f(x_sample)
    rows_p = x_prompt.shape[1] // GW
    rows_s = x_sample.shape[1] // GW
    nb_p, nb_s = x_prompt.shape[0], x_sample.shape[0]
    RP = rows_p * nb_p // 8
    RS = rows_s * nb_s // 8
    nc, geo = build_program(RP, RS, debug=debug)
    rpb = f(rpb)
    common = dict(
        rpb_exp=_bias_tables(rpb, list(range(-3, 4)), -7, 7),
        rpb_int=_bias_tables(rpb, list(range(-2, 3)), -4, 3),
        ident=np.eye(128, dtype=np.float32),
        w_in=f(w_in), w_pool_br=f(w_pool_br), w_attn_br=f(w_attn_br), w_out=f(w_out), w_up=f(w_up), w_down=f(w_down),
        poolw=np.ascontiguousarray(f(pool_w).reshape(NL, 4, 2, 128, 256).transpose(3, 0, 1, 2, 4).reshape(128, -1)),
        g1=np.ascontiguousarray(f(norm1_g).reshape(NL, KC, 128).transpose(2, 0, 1).reshape(128, -1)),
        g2=np.ascontiguousarray(f(norm2_g).reshape(NL, KC, 128).transpose(2, 0, 1).reshape(128, -1)),
        gf=np.ascontiguousarray(f(norm_f).reshape(KC, 128).T),
        pscale=np.ascontiguousarray(f(pool_scale).reshape(NL, 8, 128).transpose(2, 0, 1).reshape(128, -1)),
        convw=np.ascontiguousarray(f(conv_w).reshape(NL, 3, 88, 128).transpose(3, 0, 1, 2).reshape(128, -1)),
        convb=np.ascontiguousarray(f(conv_b).reshape(NL, 88, 128).transpose(2, 0, 1).reshape(128, -1)),
    )
    in_maps = []
    for ci in range(8):
        m = dict(common)
        m.update(_core_layout(ci, RP, RS, rows_p, rows_s, x_prompt, x_sample))
        in_maps.append(m)
    res = run_bass_kernel_spmd(nc, in_maps, core_ids=list(range(8)))
    y_p = np.empty_like(x_prompt)
    y_s = np.empty_like(x_sample)
    npq = rows_p // RP
    nsq = rows_s // RS
    for ci in range(8):
        r = res.results[ci]
        y_p[ci // npq, (ci % npq) * RP * GW:((ci % npq) + 1) * RP * GW] = r["y_p"]
        y_s[ci // nsq, (ci % nsq) * RS * GW:((ci % nsq) + 1) * RS * GW] = r["y_s"]
    if debug:
        return (y_p, y_s), res, geo
    return (y_p, y_s)


def kernel(x_prompt, x_sample, norm1_g, w_in, pool_w, pool_scale, rpb, w_pool_br, w_attn_br, w_out,
           norm2_g, w_up, conv_w, conv_b, w_down, norm_f):
    return _run(x_prompt, x_sample, norm1_g, w_in, pool_w, pool_scale, rpb, w_pool_br, w_attn_br, w_out,
                norm2_g, w_up, conv_w, conv_b, w_down, norm_f)
```
